# Optimizing a Trainium2 kernel written in Bass

```python
import math
import jax, jax.numpy as jnp
from jax import lax
import numpy as np

D_MODEL = 1024
BATCH = 2
SEQ = 8192
DEPTH = 2

HEAD_DIM = 64
D_MIX = D_MODEL
BLOCK = 128
NORM_EPS = 1e-5
A_HEADS = 4
A_QK_DIM = HEAD_DIM // 2
A_V_DIM = HEAD_DIM
A_WIDTH = A_HEADS * A_V_DIM
B_HEADS = 8
B_KV_HEADS = 2
B_GROUP = B_HEADS // B_KV_HEADS
B_WIDTH = B_HEADS * HEAD_DIM
WINDOW = 128
C_HEADS = 4
C_WIDTH = C_HEADS * HEAD_DIM
N_ALIBI_HEADS = A_HEADS + B_HEADS

SPLITS = (
    A_HEADS * 2 * A_QK_DIM, A_HEADS * 2 * A_QK_DIM, A_WIDTH, A_WIDTH,
    B_WIDTH, B_KV_HEADS * HEAD_DIM, B_KV_HEADS * HEAD_DIM, B_WIDTH,
    C_WIDTH, C_WIDTH, C_WIDTH, C_WIDTH,
)
D_IN = sum(SPLITS)
SPLIT_POINTS = tuple(int(v) for v in np.cumsum(SPLITS)[:-1])

kernel_name = "hymba_diff_swa_stickbreak_block"


def alibi_slopes():
    s = 2.0 ** (-8.0 * np.arange(1, N_ALIBI_HEADS + 1) / N_ALIBI_HEADS)
    s = s.astype(np.float32)
    return jnp.asarray(s[B_HEADS:]), jnp.asarray(s[:B_HEADS])


def rmsnorm(x, g):
    xf = x.astype(jnp.float32)
    y = xf * lax.rsqrt(jnp.mean(xf * xf, axis=-1, keepdims=True) + NORM_EPS)
    return (y * g.astype(jnp.float32)).astype(x.dtype)


def diff_attention(q, k, v, lam, slopes):
    b, s = q.shape[:2]
    nblk = s // BLOCK
    scale = A_QK_DIM ** -0.5
    kpos = jnp.arange(s)
    qb = q.reshape(b, nblk, BLOCK, A_HEADS, 2, A_QK_DIM).transpose(1, 0, 2, 3, 4, 5)

    def one_block(args):
        qblk, i = args
        qpos = i * BLOCK + jnp.arange(BLOCK)
        sc = jnp.einsum('bqhmd,bkhmd->bhmqk', qblk, k,
                        preferred_element_type=jnp.float32) * scale
        dist = (qpos[:, None] - kpos[None, :]).astype(jnp.float32)
        sc = sc - (slopes[:, None, None, None] * dist)[None]
        sc = jnp.where((dist >= 0)[None, None, None], sc, -jnp.inf)
        p = jax.nn.softmax(sc, axis=-1)
        w = p[:, :, 0] - lam * p[:, :, 1]
        return jnp.einsum('bhqk,bkhd->bqhd', w.astype(v.dtype), v)

    out = lax.map(one_block, (qb, jnp.arange(nblk)))
    return out.transpose(1, 0, 2, 3, 4).reshape(b, s, A_HEADS, A_V_DIM)


def window_attention(q, k, v, sinks, slopes):
    b, s = q.shape[:2]
    nblk = s // BLOCK
    qb = q.reshape(b, nblk, BLOCK, B_KV_HEADS, B_GROUP, HEAD_DIM)

    def banded(t):
        tb = t.reshape(b, nblk, BLOCK, B_KV_HEADS, HEAD_DIM)
        prev = jnp.pad(tb, ((0, 0), (1, 0), (0, 0), (0, 0), (0, 0)))[:, :-1]
        return jnp.concatenate([prev, tb], axis=2)

    kb, vb = banded(k), banded(v)
    sc = jnp.einsum('bnqhgd,bnkhd->bnhgqk', qb, kb,
                    preferred_element_type=jnp.float32) * HEAD_DIM ** -0.5
    qi = jnp.arange(BLOCK)
    kj = jnp.arange(2 * BLOCK) - BLOCK
    dist = qi[:, None] - kj[None, :]
    blk = jnp.arange(nblk)
    valid = ((dist >= 0) & (dist < WINDOW))[None] & \
        ((blk[:, None, None] * BLOCK + kj[None, None, :]) >= 0)
    sl = slopes.reshape(B_KV_HEADS, B_GROUP)
    sc = sc - sl[:, :, None, None] * dist.astype(jnp.float32)
    sc = jnp.where(valid[None, :, None, None], sc, -jnp.inf)
    sink_col = jnp.broadcast_to(
        sinks.reshape(B_KV_HEADS, B_GROUP)[None, None, :, :, None, None].astype(jnp.float32),
        sc.shape[:-1] + (1,))
    p = jax.nn.softmax(jnp.concatenate([sc, sink_col], axis=-1), axis=-1)[..., :-1]
    out = jnp.einsum('bnhgqk,bnkhd->bnqhgd', p.astype(v.dtype), vb)
    return out.reshape(b, s, B_HEADS, HEAD_DIM)


def stick_breaking_attention(q, k, v):
    b, s = q.shape[:2]
    nblk = s // BLOCK
    scale = HEAD_DIM ** -0.5
    kpos = jnp.arange(s)
    qb = q.reshape(b, nblk, BLOCK, C_HEADS, HEAD_DIM).transpose(1, 0, 2, 3, 4)

    def one_block(args):
        qblk, i = args
        qpos = i * BLOCK + jnp.arange(BLOCK)
        z = jnp.einsum('bqhd,bkhd->bhqk', qblk, k,
                       preferred_element_type=jnp.float32) * scale
        past = kpos[None, :] < qpos[:, None]
        log_beta = jax.nn.log_sigmoid(z)
        log_1m_beta = jnp.where(past, jax.nn.log_sigmoid(-z), 0.0)
        tail = lax.cumsum(log_1m_beta, axis=3, reverse=True) - log_1m_beta
        a = jnp.where(past, jnp.exp(log_beta + tail), 0.0)
        return jnp.einsum('bhqk,bkhd->bqhd', a.astype(v.dtype), v)

    out = lax.map(one_block, (qb, jnp.arange(nblk)))
    return out.transpose(1, 0, 2, 3, 4).reshape(b, s, C_HEADS, HEAD_DIM)


def hybrid_layer(x, norm_g, w_in, lq1, lk1, lq2, lk2, subln_g, sinks, w_out, layer_idx):
    b, s, _ = x.shape
    slopes_a, slopes_b = alibi_slopes()
    h = rmsnorm(x, norm_g)
    proj = jnp.einsum('bsd,de->bse', h, w_in)
    (aq, ak, av, ag, bq, bk, bv, bg, cq, ck, cv, cg) = jnp.split(proj, SPLIT_POINTS, axis=-1)

    lambda_init = 0.8 - 0.6 * math.exp(-0.3 * layer_idx)
    lam = (jnp.exp(jnp.sum(lq1.astype(jnp.float32) * lk1.astype(jnp.float32)))
           - jnp.exp(jnp.sum(lq2.astype(jnp.float32) * lk2.astype(jnp.float32)))
           + lambda_init)
    ya = diff_attention(aq.reshape(b, s, A_HEADS, 2, A_QK_DIM),
                        ak.reshape(b, s, A_HEADS, 2, A_QK_DIM),
                        av.reshape(b, s, A_HEADS, A_V_DIM), lam, slopes_a)
    ya = rmsnorm(ya, subln_g) * (1.0 - lambda_init)
    ya = ya.reshape(b, s, A_WIDTH) * jax.nn.silu(ag)

    yb = window_attention(bq.reshape(b, s, B_HEADS, HEAD_DIM),
                          bk.reshape(b, s, B_KV_HEADS, HEAD_DIM),
                          bv.reshape(b, s, B_KV_HEADS, HEAD_DIM), sinks, slopes_b)
    yb = yb.reshape(b, s, B_WIDTH) * jax.nn.silu(bg)

    yc = stick_breaking_attention(cq.reshape(b, s, C_HEADS, HEAD_DIM),
                                  ck.reshape(b, s, C_HEADS, HEAD_DIM),
                                  cv.reshape(b, s, C_HEADS, HEAD_DIM))
    yc = yc.reshape(b, s, C_WIDTH) * jax.nn.silu(cg)

    y = jnp.concatenate([ya, yb, yc], axis=-1)
    return x + jnp.einsum('bse,ed->bsd', y, w_out)


def setup_inputs(seed: int = 0) -> dict:
    key = jax.random.key(seed)
    ks = jax.random.split(key, 12)
    f32 = jnp.float32
    return {
        "x": jax.random.normal(ks[0], (BATCH, SEQ, D_MODEL), f32),
        "norm_g": 1.0 + 0.02 * jax.random.normal(ks[1], (DEPTH, D_MODEL), f32),
        "w_in": jax.random.normal(ks[2], (DEPTH, D_MODEL, D_IN), f32) * D_MODEL ** -0.5,
        "lambda_q1": 0.1 * jax.random.normal(ks[3], (DEPTH, A_QK_DIM), f32),
        "lambda_k1": 0.1 * jax.random.normal(ks[4], (DEPTH, A_QK_DIM), f32),
        "lambda_q2": 0.1 * jax.random.normal(ks[5], (DEPTH, A_QK_DIM), f32),
        "lambda_k2": 0.1 * jax.random.normal(ks[6], (DEPTH, A_QK_DIM), f32),
        "subln_g": 1.0 + 0.02 * jax.random.normal(ks[7], (DEPTH, A_V_DIM), f32),
        "sinks": 0.5 * jax.random.normal(ks[8], (DEPTH, B_HEADS), f32),
        "w_out": jax.random.normal(ks[9], (DEPTH, D_MIX, D_MODEL), f32) * D_MIX ** -0.5,
        "final_g": 1.0 + 0.02 * jax.random.normal(ks[10], (D_MODEL,), f32),
    }


def reference(x, norm_g, w_in, lambda_q1, lambda_k1, lambda_q2, lambda_k2, subln_g, sinks, w_out, final_g):
    for l in range(DEPTH):
        x = hybrid_layer(x, norm_g[l], w_in[l], lambda_q1[l], lambda_k1[l], lambda_q2[l],
                         lambda_k2[l], subln_g[l], sinks[l], w_out[l], l)
    return rmsnorm(x, final_g)
```

```python
import math
import contextlib
import numpy as np
import ml_dtypes
import concourse.bass as bass
import concourse.mybir as mybir
from concourse.bass_utils import run_bass_kernel_spmd

F32 = mybir.dt.float32
BF16 = mybir.dt.bfloat16
AF = mybir.ActivationFunctionType
ALU = mybir.AluOpType
AX = mybir.AxisListType
NPBF = ml_dtypes.bfloat16

D = 1024
SEQ = 8192
DEPTH = 2
DIN = 3328
NSLOT = 16
TOK = 2048
EPS = 1e-5
NEG = -30000.0
KTROWS = 640
VW = 688
VA0, VB0, VC0 = 0, 288, 432
AQ, AK, AV, AG = 0, 256, 512, 768
BQ, BK, BV, BG = 1024, 1536, 1664, 1792
CQ, CK, CV, CG = 2304, 2560, 2816, 3072


def pos_of(kb):
    m, q = divmod(kb, 8)
    if q < 4:
        r, j = q, 2 * m
    else:
        r, j = 7 - q, 2 * m + 1
    return r * 16 + j


def qb_of(r, j):
    m, s = divmod(j, 2)
    return 8 * m + (r if s == 0 else 7 - r)


def kb_of_pos(pos):
    r, j = divmod(pos, 16)
    return qb_of(r, j)


def split3(v):
    v = np.asarray(v, np.float64)
    hi = v.astype(NPBF)
    mid = (v - hi.astype(np.float64)).astype(NPBF)
    lo = (v - hi.astype(np.float64) - mid.astype(np.float64)).astype(NPBF)
    return hi, mid, lo


def alibi():
    s = (2.0 ** (-8.0 * np.arange(1, 13) / 12)).astype(np.float32)
    return s[8:].astype(np.float64), s[:8].astype(np.float64)


def host_tables(r):
    sa, sb = alibi()
    t = {}
    kpos = np.zeros(8192)
    for pos in range(64):
        kpos[pos * 128:(pos + 1) * 128] = kb_of_pos(pos) * 128 + np.arange(128)
    qpos = np.zeros(2048)
    for j in range(16):
        qpos[j * 128:(j + 1) * 128] = qb_of(r, j) * 128 + np.arange(128)
    one_k = np.ones(8192, NPBF)
    one_q = np.ones(2048, NPBF)
    ka = np.zeros((4, 6, 8192), NPBF)
    qa = np.zeros((4, 6, 2048), NPBF)
    for h in range(4):
        w = split3(sa[h] * kpos)
        u = split3(-sa[h] * qpos)
        for i in range(3):
            ka[h, i] = one_k
            ka[h, 3 + i] = w[i]
            qa[h, i] = u[i]
            qa[h, 3 + i] = one_q
    t["kaugA"] = ka
    t["qaugA"] = qa
    kbase = (kpos // 128) * 128
    ki = kpos % 128
    kb_ = np.zeros((9, 8192), NPBF)
    kb_[0:3] = 1
    for i in range(3):
        kb_[3 + 2 * i] = kbase.astype(NPBF)
        kb_[4 + 2 * i] = ki.astype(NPBF)
    t["kaugB"] = kb_
    qbt = np.zeros((2, 9, 16, 4, 128), NPBF)
    for g2 in range(2):
        for hh in range(4):
            s = sb[g2 * 4 + hh]
            u = split3((-s * qpos).reshape(16, 128))
            s3 = split3(np.array([s]))
            for i in range(3):
                qbt[g2, i, :, hh, :] = u[i]
                qbt[g2, 3 + 2 * i, :, hh, :] = s3[i][0]
                qbt[g2, 4 + 2 * i, :, hh, :] = s3[i][0]
    t["qaugB"] = qbt.reshape(2, 9, 8192)
    kk = np.arange(128)[:, None]
    qq = np.arange(128)[None, :]
    mA = np.zeros((128, 8, 128), np.float32)
    mC = np.zeros((128, 8, 128), np.float32)
    for s in range(2):
        qr = r if s == 0 else 3 - r
        for rel in range(4):
            if rel == qr:
                mA[:, s * 4 + rel, :] = np.where(kk > qq, NEG, 0.0)
                mC[:, s * 4 + rel, :] = np.where(kk >= qq, NEG, 0.0)
            elif rel > qr:
                mA[:, s * 4 + rel, :] = NEG
                mC[:, s * 4 + rel, :] = NEG
    t["maskA"] = mA.reshape(128, 1024).astype(NPBF)
    t["maskC"] = mC.reshape(128, 1024).astype(NPBF)
    t["m01C"] = (mC == 0).astype(np.float32).reshape(128, 1024).astype(NPBF)
    mB = np.full((128, 10, 128), NEG, np.float32)
    for s in range(2):
        qr = r if s == 0 else 3 - r
        for rel in range(5):
            delta = qr - rel + 1
            if delta == 0:
                mB[:, s * 5 + rel, :] = np.where(kk > qq, NEG, 0.0)
            elif delta == 1:
                mB[:, s * 5 + rel, :] = np.where(kk > qq, 0.0, NEG)
    t["maskB"] = mB.reshape(128, 1280).astype(NPBF)
    t["ident"] = np.eye(128, dtype=np.float32).astype(NPBF)
    t["negU"] = np.where(kk >= qq, -1.0, 0.0).astype(NPBF)
    sel = np.zeros((64, 128), np.float32)
    sel[0, :] = -1.0
    sel[32, :] = -1.0
    t["negSel"] = sel.astype(NPBF)
    t["ones64"] = np.ones((128, 64), NPBF)
    return t


class Buf:
    __slots__ = ("w", "r")

    def __init__(self):
        self.w = {}
        self.r = {}


class Builder:
    def __init__(self):
        self.nc = bass.Bass("TRN2", target_bir_lowering=False)
        nc = self.nc
        self.eng = {"pe": nc.tensor, "act": nc.scalar, "dve": nc.vector, "pool": nc.gpsimd, "sp": nc.sync}
        self.esem = {}
        self.ecnt = {}
        for e in ("pe", "act", "dve", "pool"):
            self.esem[e] = nc.semaphore("es_" + e).__enter__()
            self.ecnt[e] = 0
        self.ND = 48
        self.dsem = [nc.semaphore("ds%d" % i).__enter__() for i in range(self.ND)]
        self.dcnt = [0] * self.ND
        self.dnext = 0
        self.waited = {e: {} for e in self.eng}
        self.nt = 0
        self.stack = contextlib.ExitStack()

    def sb(self, shape, dt, name=None):
        self.nt += 1
        return self.stack.enter_context(self.nc.sbuf_tensor(name or "t%d" % self.nt, list(shape), dt))

    def dram(self, name, shape, dt, kind="Internal"):
        return self.nc.dram_tensor(name, list(shape), dt, kind=kind).ap()

    def _wait(self, e, tok):
        if tok is None:
            return
        name, sem, val, prod = tok
        if prod == "pe" and e == "pe":
            return
        if self.waited[e].get(name, 0) >= val:
            return
        self.eng[e].wait_ge(sem, val)
        self.waited[e][name] = val

    def _deps(self, e, reads, writes, part=False):
        for b in reads:
            for t in b.w.values():
                self._wait(e, t)
        for b in writes:
            for t in b.r.values():
                self._wait(e, t)
            if not part:
                for t in b.w.values():
                    self._wait(e, t)

    def _mark(self, tok, reads, writes, part=False):
        key = tok[3] + tok[0]
        for b in reads:
            b.r[key] = tok
        for b in writes:
            if part:
                b.w[key] = tok
            else:
                b.w = {key: tok}
                b.r = {}

    def op(self, e, fn, reads=(), writes=()):
        self._deps(e, reads, writes)
        ins = fn(self.eng[e])
        self.ecnt[e] += 1
        ins.then_inc(self.esem[e], 1)
        tok = ("es_" + e, self.esem[e], self.ecnt[e], e)
        self._mark(tok, reads, writes)
        return tok

    def pe_quiet(self, fn):
        fn(self.eng["pe"])

    def dma(self, q, out, in_, reads=(), writes=(), part=False):
        self._deps(q, reads, writes, part)
        i = self.dnext
        self.dnext = (i + 1) % self.ND
        name = "ds%d" % i
        if self.dcnt[i] > 0:
            self._wait(q, (name, self.dsem[i], self.dcnt[i], "dma"))
        ins = self.eng[q].dma_start(out=out, in_=in_)
        self.dcnt[i] += 16
        ins.then_inc(self.dsem[i], 16)
        tok = (name, self.dsem[i], self.dcnt[i], "dma")
        self._mark(tok, reads, writes, part)
        return tok

    def barrier(self):
        toks = []
        for e in ("pe", "act", "dve", "pool"):
            if self.ecnt[e] > 0:
                toks.append(("es_" + e, self.esem[e], self.ecnt[e], "x"))
        for i in range(self.ND):
            if self.dcnt[i] > 0:
                toks.append(("ds%d" % i, self.dsem[i], self.dcnt[i], "dma"))
        for e in self.eng:
            for t in toks:
                if t[0] == "es_" + e:
                    continue
                self._wait(e, t)


def build(stage, r_dummy=None):
    B = Builder()
    nc = B.nc
    fused = stage == 4
    nlay_attn = {1: 0, 2: 1, 3: 2, 4: 2}[stage]
    nlay_proj = {1: 1, 2: 2, 3: 2, 4: 2}[stage]

    def ext_in(name, shape, dt):
        return B.dram(name, shape, dt, kind="ExternalInput")

    x_d = ext_in("x", [NSLOT, 128, D], F32)
    win_d = ext_in("w_in", [DEPTH, D, DIN], F32)
    wout_d = ext_in("w_out", [DEPTH, D, D], F32)
    ng_d = ext_in("ng", [DEPTH, 128, D], F32)
    fg_d = ext_in("fg", [128, D], F32)
    lam_d = ext_in("lam4", [DEPTH, 128, 128], F32)
    subg_d = ext_in("subg", [DEPTH, 128, 64], F32)
    sink_d = ext_in("sinks", [DEPTH, 128, 8], F32)
    ident_d = ext_in("ident", [128, 128], BF16)
    kaugA_d = ext_in("kaugA", [4, 6, 8192], BF16)
    qaugA_d = ext_in("qaugA", [4, 6, 2048], BF16)
    kaugB_d = ext_in("kaugB", [9, 8192], BF16)
    qaugB_d = ext_in("qaugB", [2, 9, 8192], BF16)
    maskA_d = ext_in("maskA", [128, 1024], BF16)
    maskC_d = ext_in("maskC", [128, 1024], BF16)
    m01C_d = ext_in("m01C", [128, 1024], BF16)
    maskB_d = ext_in("maskB", [128, 1280], BF16)
    negU_d = ext_in("negU", [128, 128], BF16)
    negSel_d = ext_in("negSel", [64, 128], BF16)
    ones64_d = ext_in("ones64", [128, 64], BF16)

    ktl_d, vl_d, kta_d, va_d = [], [], [], []
    for l in range(DEPTH):
        last_proj = (l == nlay_proj - 1) and stage in (1, 2)
        ktl_d.append(B.dram("ktl%d" % l, [KTROWS, TOK], BF16, kind="ExternalOutput" if last_proj else "Internal"))
        vl_d.append(B.dram("vl%d" % l, [TOK, VW], BF16, kind="ExternalOutput" if last_proj else "Internal"))
        if fused:
            kta_d.append(B.dram("kta%d" % l, [4 * KTROWS, TOK], BF16))
            va_d.append(B.dram("va%d" % l, [4 * TOK, VW], BF16))
        elif l < nlay_attn:
            kta_d.append(ext_in("kta%d" % l, [4 * KTROWS, TOK], BF16))
            va_d.append(ext_in("va%d" % l, [4 * TOK, VW], BF16))
    ql_d = B.dram("ql", [1024, TOK], BF16, kind="ExternalOutput" if stage == 1 else "Internal")
    xdbg_d = B.dram("xdbg", [NSLOT, 128, D], F32, kind="ExternalOutput") if stage == 2 else None
    out_d = None
    if stage >= 3:
        out_d = B.dram("out", [NSLOT, 128, D], F32, kind="ExternalOutput")

    x_sb = B.sb([128, NSLOT, D], F32, "x_sb")
    gy = B.sb([128, NSLOT, D], BF16, "gy")
    ident = B.sb([128, 128], BF16, "ident_sb")
    xB = [Buf() for _ in range(NSLOT)]
    gyB = [Buf() for _ in range(NSLOT)]
    identB = Buf()
    ps = [nc.psum_tensor("ps%d" % i, [128, 512], F32).__enter__() for i in range(8)]
    psB = [Buf() for _ in range(8)]
    qlB = Buf()
    ktlB = [Buf() for _ in range(DEPTH)]
    vlB = [Buf() for _ in range(DEPTH)]
    ktaB = [Buf() for _ in range(DEPTH)]
    vaB = [Buf() for _ in range(DEPTH)]

    B.dma("sp", ident[:, :], ident_d[:, :], writes=[identB])
    for j in range(NSLOT):
        B.dma("sp", x_sb[:, j, :], x_d[j, :, :], writes=[xB[j]])

    sa, sb_sl = alibi()

    def rms_to_hT(l, hT, hTB, gsrc):
        gbc = B.sb([128, D], F32)
        gB = Buf()
        B.dma("sp", gbc[:, :], gsrc, writes=[gB])
        ssq = B.sb([128, NSLOT], F32)
        rstd = B.sb([128, NSLOT], F32)
        junk = B.sb([128, D], F32)
        junkB = Buf()
        ssqB = Buf()
        rstdB = Buf()
        hb = [B.sb([128, D], BF16) for _ in range(2)]
        hbB = [Buf(), Buf()]
        for j in range(NSLOT):
            B.op("act", lambda e, j=j: e.activation(out=junk[:, :], in_=x_sb[:, j, :], func=AF.Square,
                                                    accum_out=ssq[:, j:j + 1]),
                 reads=[xB[j]], writes=[junkB, ssqB])
        B.op("act", lambda e: e.activation(out=rstd[:, :], in_=ssq[:, :], func=AF.Ln, scale=1.0 / D, bias=EPS),
             reads=[ssqB], writes=[rstdB])
        B.op("act", lambda e: e.activation(out=rstd[:, :], in_=rstd[:, :], func=AF.Exp, scale=-0.5),
             reads=[rstdB], writes=[rstdB])
        for j in range(NSLOT):
            b = j % 2
            B.op("dve", lambda e, j=j, b=b: e.scalar_tensor_tensor(out=hb[b][:, :], in0=x_sb[:, j, :],
                                                                   scalar=rstd[:, j:j + 1], in1=gbc[:, :],
                                                                   op0=ALU.mult, op1=ALU.mult),
                 reads=[xB[j], rstdB, gB], writes=[hbB[b]])
            pb = 6 + b
            tp = ps[pb][:, :].bitcast(BF16)
            for c in range(8):
                fn = lambda e, c=c, b=b, tp=tp: e.transpose(out=tp[:, c * 128:(c + 1) * 128],
                                                           in_=hb[b][:, c * 128:(c + 1) * 128],
                                                           identity=ident[:, :])
                if c == 0:
                    B._deps("pe", [hbB[b], identB], [psB[pb]])
                if c < 7:
                    B.pe_quiet(fn)
                else:
                    B.op("pe", fn, reads=[hbB[b], identB], writes=[psB[pb]])
            eng = "act" if j % 2 == 0 else "dve"
            if eng == "act":
                B.op("act", lambda e, j=j, tp=tp: e.copy(out=hT[:, :, j * 128:(j + 1) * 128],
                                                         in_=tp.rearrange("p (c t) -> p c t", c=8)),
                     reads=[psB[pb]], writes=[hTB])
            else:
                B.op("dve", lambda e, j=j, tp=tp: e.tensor_copy(out=hT[:, :, j * 128:(j + 1) * 128],
                                                                in_=tp.rearrange("p (c t) -> p c t", c=8)),
                     reads=[psB[pb]], writes=[hTB])

    def phase_proj(l):
        B.barrier()
        old_stack = B.stack
        B.stack = contextlib.ExitStack()
        hT = B.sb([128, 8, TOK], BF16)
        hTB = Buf()
        rms_to_hT(l, hT, hTB, ng_d[l, :, :])
        wst = [B.sb([128, 8, 256], F32) for _ in range(2)]
        wbf = [B.sb([128, 8, 256], BF16) for _ in range(2)]
        wstB = [Buf(), Buf()]
        wbfB = [Buf(), Buf()]
        fst = [B.sb([128, TOK], BF16) for _ in range(2)]
        fstB = [Buf(), Buf()]
        vst = B.sb([128, NSLOT, 288], BF16)
        vstB = Buf()
        win_v = win_d[l, :, :].rearrange("(c p) n -> p c n", p=128)
        chunks = []
        chunks.append((AQ, [("F", "q", 0, 32 ** -0.5), ("F", "q", 128, 32 ** -0.5)]))
        chunks.append((AK, [("F", "k", 0, 1.0), ("F", "k", 128, 1.0)]))
        chunks.append((AV, [("V", VA0, 4, 72, 256)]))
        chunks.append((AG, [("G", 0, 256)]))
        chunks.append((BQ, [("F", "q", 256, 0.125), ("F", "q", 384, 0.125)]))
        chunks.append((BQ + 256, [("F", "q", 512, 0.125), ("F", "q", 640, 0.125)]))
        chunks.append((BK, [("F", "k", 256, 1.0), ("V1", VB0, 2, 72, 128)]))
        chunks.append((BG, [("G", 256, 256)]))
        chunks.append((BG + 256, [("G", 512, 256)]))
        chunks.append((CQ, [("F", "q", 768, 0.125), ("F", "q", 896, 0.125)]))
        chunks.append((CK, [("F", "k", 384, 1.0), ("F", "k", 512, 1.0)]))
        chunks.append((CV, [("V", VC0, 4, 64, 256)]))
        chunks.append((CG, [("G", 768, 256)]))
        fcount = 0
        pcount = 0
        vinit = False
        for ci, (col0, kinds) in enumerate(chunks):
            b = ci % 2
            B.dma("sp", wst[b][:, :, :], win_v[:, :, col0:col0 + 256], writes=[wstB[b]])
            B.op("pool", lambda e, b=b: e.tensor_copy(out=wbf[b][:, :, :], in_=wst[b][:, :, :]),
                 reads=[wstB[b]], writes=[wbfB[b]])
            for hi, kd in enumerate(kinds):
                if kd[0] == "F":
                    _, dst, row0, scale = kd
                    c0 = hi * 128
                    fb = fcount % 2
                    fcount += 1
                    for tg in range(4):
                        pb = pcount % 4
                        pcount += 1
                        B._deps("pe", [wbfB[b], hTB], [psB[pb]])
                        for c in range(8):
                            fn = lambda e, c=c, pb=pb, b=b, c0=c0, tg=tg: e.matmul(
                                ps[pb][:, :], lhsT=wbf[b][:, c, c0:c0 + 128], rhs=hT[:, c, tg * 512:(tg + 1) * 512],
                                start=(c == 0), stop=(c == 7))
                            if c < 7:
                                B.pe_quiet(fn)
                            else:
                                B.op("pe", fn, reads=[wbfB[b], hTB], writes=[psB[pb]])
                        B.op("dve", lambda e, pb=pb, fb=fb, tg=tg, scale=scale: e.tensor_scalar(
                            out=fst[fb][:, tg * 512:(tg + 1) * 512], in0=ps[pb][:, :], scalar1=float(scale),
                            scalar2=None, op0=ALU.mult),
                            reads=[psB[pb]], writes=[fstB[fb]])
                    if dst == "q":
                        B.dma("pool", ql_d[row0:row0 + 128, :], fst[fb][:, :], reads=[fstB[fb]], writes=[qlB])
                    else:
                        B.dma("pool", ktl_d[l][row0:row0 + 128, :], fst[fb][:, :], reads=[fstB[fb]],
                              writes=[ktlB[l]])
                elif kd[0] in ("V", "V1"):
                    _, vcol0, nh, hw, width = kd
                    c0 = 128 if kd[0] == "V1" else 0
                    if hw == 72:
                        B.op("pool", lambda e, nh=nh: e.memset(vst[:, :, 0:nh * 72], 1.0), writes=[vstB])
                    for j in range(NSLOT):
                        pb = pcount % 4
                        pcount += 1
                        B._deps("pe", [wbfB[b], hTB], [psB[pb]])
                        for c in range(8):
                            fn = lambda e, c=c, pb=pb, b=b, c0=c0, j=j, width=width: e.matmul(
                                ps[pb][:, 0:width], lhsT=hT[:, c, j * 128:(j + 1) * 128],
                                rhs=wbf[b][:, c, c0:c0 + width], start=(c == 0), stop=(c == 7))
                            if c < 7:
                                B.pe_quiet(fn)
                            else:
                                B.op("pe", fn, reads=[wbfB[b], hTB], writes=[psB[pb]])
                        B.op("dve", lambda e, pb=pb, j=j, nh=nh, hw=hw, width=width: e.tensor_copy(
                            out=vst[:, j, 0:nh * hw].rearrange("p (h w) -> p h w", h=nh)[:, :, 0:64],
                            in_=ps[pb][:, 0:width].rearrange("p (h w) -> p h w", h=nh)),
                            reads=[psB[pb]], writes=[vstB])
                    B.dma("pool", vl_d[l][:, vcol0:vcol0 + nh * hw].rearrange("(j p) w -> p j w", p=128),
                          vst[:, :, 0:nh * hw], reads=[vstB], writes=[vlB[l]])
                else:
                    _, gcol0, width = kd
                    for j in range(NSLOT):
                        pb = pcount % 4
                        pcount += 1
                        B._deps("pe", [wbfB[b], hTB], [psB[pb]])
                        for c in range(8):
                            fn = lambda e, c=c, pb=pb, b=b, j=j, width=width: e.matmul(
                                ps[pb][:, 0:width], lhsT=hT[:, c, j * 128:(j + 1) * 128],
                                rhs=wbf[b][:, c, 0:width], start=(c == 0), stop=(c == 7))
                            if c < 7:
                                B.pe_quiet(fn)
                            else:
                                B.op("pe", fn, reads=[wbfB[b], hTB], writes=[psB[pb]])
                        B.op("act", lambda e, pb=pb, j=j, gcol0=gcol0, width=width: e.activation(
                            out=gy[:, j, gcol0:gcol0 + width], in_=ps[pb][:, 0:width], func=AF.Silu),
                            reads=[psB[pb]], writes=[gyB[j]])
        B.barrier()
        B.stack.close()
        B.stack = old_stack

    def gather(l):
        if not fused:
            return
        B._deps("pool", [ktlB[l], vlB[l]], [ktaB[l], vaB[l]])
        for src, dst in ((ktl_d[l], kta_d[l]), (vl_d[l], va_d[l])):
            ins = nc.gpsimd.collective_compute("AllGather", ALU.bypass,
                                               replica_groups=[[0, 1, 2, 3], [4, 5, 6, 7]],
                                               ins=[src[:, :]], outs=[dst[:, :]])
            B.ecnt["pool"] += 1
            ins.then_inc(B.esem["pool"], 1)
        tok = ("es_pool", B.esem["pool"], B.ecnt["pool"], "pool")
        B._mark(tok, [ktlB[l], vlB[l]], [ktaB[l], vaB[l]])

    def slot_info(g):
        res = []
        for t in range(4):
            j = 4 * g + t
            m, s = divmod(j, 2)
            L = 8 * m + 4 + 4 * s
            res.append((t, j, L, s))
        return res

    def phase_attn(l):
        B.barrier()
        old_stack = B.stack
        B.stack = contextlib.ExitStack()
        lambda_init = 0.8 - 0.6 * math.exp(-0.3 * l)
        maskA = B.sb([128, 1024], BF16)
        maskC = B.sb([128, 1024], BF16)
        m01C = B.sb([128, 1024], BF16)
        maskB = B.sb([128, 1280], BF16)
        negU = B.sb([128, 128], BF16)
        negSel = B.sb([64, 128], BF16)
        ones64 = B.sb([128, 64], BF16)
        cB = Buf()
        for dst, src in ((maskA, maskA_d), (maskC, maskC_d), (m01C, m01C_d), (maskB, maskB_d), (negU, negU_d),
                         (negSel, negSel_d), (ones64, ones64_d)):
            B.dma("sp", dst[:, :], src[:, :], writes=[cB], part=True)
        lamt = B.sb([128, 128], F32)
        subg = B.sb([128, 64], F32)
        sinkt = B.sb([128, 8], F32)
        B.dma("sp", lamt[:, :], lam_d[l, :, :], writes=[cB], part=True)
        B.dma("sp", subg[:, :], subg_d[l, :, :], writes=[cB], part=True)
        B.dma("sp", sinkt[:, :], sink_d[l, :, :], writes=[cB], part=True)
        sm = B.sb([128, 16], F32)
        smB = Buf()
        prod = B.sb([128, 64], F32)
        B.op("dve", lambda e: e.tensor_tensor(out=prod[:, 0:32], in0=lamt[:, 0:32], in1=lamt[:, 32:64], op=ALU.mult),
             reads=[cB], writes=[smB])
        B.op("dve", lambda e: e.tensor_tensor(out=prod[:, 32:64], in0=lamt[:, 64:96], in1=lamt[:, 96:128],
                                              op=ALU.mult), reads=[cB, smB], writes=[smB])
        B.op("dve", lambda e: e.reduce_sum(out=sm[:, 0:1], in_=prod[:, 0:32], axis=AX.X), reads=[smB], writes=[smB])
        B.op("dve", lambda e: e.reduce_sum(out=sm[:, 1:2], in_=prod[:, 32:64], axis=AX.X), reads=[smB], writes=[smB])
        B.op("act", lambda e: e.activation(out=sm[:, 2:4], in_=sm[:, 0:2], func=AF.Exp), reads=[smB], writes=[smB])
        B.op("dve", lambda e: e.tensor_tensor(out=sm[:, 4:5], in0=sm[:, 3:4], in1=sm[:, 2:3], op=ALU.subtract),
             reads=[smB], writes=[smB])
        B.op("dve", lambda e: e.tensor_scalar(out=sm[:, 4:5], in0=sm[:, 4:5], scalar1=-float(lambda_init),
                                              scalar2=None, op0=ALU.add), reads=[smB], writes=[smB])
        B.op("dve", lambda e: e.tensor_scalar(out=subg[:, :], in0=subg[:, :], scalar1=float(1.0 - lambda_init),
                                              scalar2=None, op0=ALU.mult), reads=[cB, smB], writes=[smB])
        B.op("act", lambda e: e.activation(out=sm[:, 8:16], in_=sinkt[:, :], func=AF.Exp), reads=[cB, smB],
             writes=[smB])
        constB = [cB, smB, identB]

        kta = kta_d[l]
        va = va_d[l]
        KT = [B.sb([128, 8192], BF16) for _ in range(2)]
        KTB = [Buf(), Buf()]
        QT = [B.sb([128, TOK], BF16) for _ in range(2)]
        QTB = [Buf(), Buf()]
        Vt = [B.sb([128, 64, 128], BF16) for _ in range(2)]
        VtB = [Buf(), Buf()]
        PT = [B.sb([128, 512], BF16) for _ in range(4)]
        PTB = [Buf() for _ in range(4)]
        Et = [B.sb([128, 512], F32) for _ in range(2)]
        EtB = [Buf(), Buf()]
        SPt = [B.sb([128, 512], BF16) for _ in range(2)]
        SPB = [Buf(), Buf()]
        Rt = B.sb([64, 512], BF16)
        RB = Buf()
        ep = B.sb([128, 8, 64], F32)
        epB = Buf()
        es = B.sb([128, 16], F32)
        esB = Buf()
        hcount = [0]

        def load_head(kind, h):
            if kind == "B":
                b = 0
            else:
                b = hcount[0] % 2
                hcount[0] += 1
            kt, qt, vt = KT[b], QT[b], Vt[b]
            ktv = kta.rearrange("(r w) n -> w r n", r=4)
            vav = va.rearrange("(r j p) w -> p (r j) w", r=4, p=128)
            ktd = lambda rows: kt[rows, :].rearrange("p (r n) -> p r n", r=4)
            if kind == "A":
                for m in range(2):
                    r0 = (h * 2 + m) * 32
                    B.dma("sp", ktd(slice(m * 64, m * 64 + 32)), ktv[r0:r0 + 32, :, :], reads=[ktaB[l]],
                          writes=[KTB[b]], part=True)
                    B.dma("sp", kt[m * 64 + 32:m * 64 + 38, :], kaugA_d[h, :, :], writes=[KTB[b]], part=True)
                    B.dma("sp", qt[m * 64:m * 64 + 32, :], ql_d[r0:r0 + 32, :], reads=[qlB], writes=[QTB[b]], part=True)
                    B.dma("sp", qt[m * 64 + 32:m * 64 + 38, :], qaugA_d[h, :, :], writes=[QTB[b]], part=True)
                B.dma("sp", vt[:, :, 0:72], vav[:, :, VA0 + h * 72:VA0 + (h + 1) * 72], reads=[vaB[l]],
                      writes=[VtB[b]], part=True)
            elif kind == "C":
                for m in range(2):
                    hh = h * 2 + m
                    B.dma("sp", ktd(slice(m * 64, m * 64 + 64)), ktv[384 + hh * 64:384 + (hh + 1) * 64, :, :],
                          reads=[ktaB[l]], writes=[KTB[b]], part=True)
                    B.dma("sp", qt[m * 64:m * 64 + 64, :], ql_d[768 + hh * 64:768 + (hh + 1) * 64, :], reads=[qlB],
                          writes=[QTB[b]], part=True)
                B.dma("sp", vt[:, :, 0:128], vav[:, :, VC0 + h * 128:VC0 + (h + 1) * 128], reads=[vaB[l]],
                      writes=[VtB[b]], part=True)
            else:
                qx = KT[1]
                B.dma("sp", ktd(slice(0, 64)), ktv[256 + h * 64:256 + (h + 1) * 64, :, :], reads=[ktaB[l]],
                      writes=[KTB[0]], part=True)
                B.dma("sp", kt[64:73, :], kaugB_d[:, :], writes=[KTB[0]], part=True)
                for hh in range(4):
                    r0 = 256 + (h * 4 + hh) * 64
                    B.dma("sp", qx[0:64, :].rearrange("p (j hh i) -> p j hh i", j=16, hh=4)[:, :, hh, :],
                          ql_d[r0:r0 + 64, :].rearrange("p (j i) -> p j i", j=16), reads=[qlB], writes=[KTB[1]], part=True)
                B.dma("sp", qx[64:73, :], qaugB_d[h, :, :], writes=[KTB[1]], part=True)
                B.dma("sp", vt[:, :, 0:72], vav[:, :, VB0 + h * 72:VB0 + (h + 1) * 72], reads=[vaB[l]],
                      writes=[VtB[0]], part=True)
            return b

        stepc = [0]
        grpc = [0]

        def run_A(h):
            b = load_head("A", h)
            kt, qt, vt = KT[b], QT[b], Vt[b]
            for g in range(4):
                par = grpc[0] % 2
                grpc[0] += 1
                slots = slot_info(g)
                nk = slots[3][2]
                Ob = [4 + 2 * par, 5 + 2 * par]
                first_pv = [True, True]
                for kb in range(nk):
                    sbuf = stepc[0] % 2
                    stepc[0] += 1
                    act_slots = [s_ for s_ in slots if kb < s_[2]]
                    t0 = act_slots[0][0]
                    pos = pos_of(kb)
                    for m in range(2):
                        pb = 2 * sbuf + m
                        near = [s_ for s_ in act_slots if kb >= s_[2] - 4]
                        B._deps("pe", [KTB[b], QTB[b]] + constB, [psB[pb]])
                        fn = lambda e, m=m, pb=pb, pos=pos, t0=t0, g=g, last=(len(near) == 0): e.matmul(
                            ps[pb][:, t0 * 128:512], lhsT=kt[m * 64:m * 64 + 38, pos * 128:(pos + 1) * 128],
                            rhs=qt[m * 64:m * 64 + 38, (4 * g + t0) * 128:(4 * g + 4) * 128], start=True, stop=last)
                        if near:
                            B.pe_quiet(fn)
                            for ni, (t, j, L, s) in enumerate(near):
                                rel = kb - (L - 4)
                                mi = s * 4 + rel
                                fn2 = lambda e, pb=pb, t=t, mi=mi, last=(ni == len(near) - 1): e.matmul(
                                    ps[pb][:, t * 128:(t + 1) * 128], lhsT=ident[:, :],
                                    rhs=maskA[:, mi * 128:(mi + 1) * 128], start=False, stop=last)
                                if ni < len(near) - 1:
                                    B.pe_quiet(fn2)
                                else:
                                    B.op("pe", fn2, reads=[KTB[b], QTB[b]] + constB, writes=[psB[pb]])
                        else:
                            B.op("pe", fn, reads=[KTB[b], QTB[b]] + constB, writes=[psB[pb]])
                        pti = 2 * sbuf + m
                        B.op("act", lambda e, pb=pb, pti=pti, t0=t0: e.activation(
                            out=PT[pti][:, t0 * 128:512], in_=ps[pb][:, t0 * 128:512], func=AF.Exp),
                            reads=[psB[pb]], writes=[PTB[pti]])
                    for m in range(2):
                        pti = 2 * sbuf + m
                        ob = Ob[m]
                        B._deps("pe", [PTB[pti], VtB[b]], [psB[ob]])
                        for ai, (t, j, L, s) in enumerate(act_slots):
                            fn = lambda e, pti=pti, ob=ob, t=t, pos=pos, st=first_pv[m], sp_=(kb == L - 1): e.matmul(
                                ps[ob][:, t * 72:t * 72 + 65], lhsT=PT[pti][:, t * 128:(t + 1) * 128],
                                rhs=vt[:, pos, 0:65], start=st, stop=sp_, skip_group_check=True)
                            first_pv[m] = False
                            if ai < len(act_slots) - 1:
                                B.pe_quiet(fn)
                            else:
                                B.op("pe", fn, reads=[PTB[pti], VtB[b]], writes=[psB[ob]])
                O0, O1 = ps[Ob[0]], ps[Ob[1]]
                rd = [psB[Ob[0]], psB[Ob[1]]] + constB
                o0v = O0[:, 0:288].rearrange("p (t w) -> p t w", t=4)
                o1v = O1[:, 0:288].rearrange("p (t w) -> p t w", t=4)
                B.op("dve", lambda e: e.reciprocal(out=es[:, 0:4], in_=o0v[:, :, 64]), reads=rd, writes=[esB])
                B.op("dve", lambda e: e.reciprocal(out=es[:, 4:8], in_=o1v[:, :, 64]), reads=rd + [esB], writes=[esB])
                for t in range(4):
                    j = 4 * g + t
                    B.op("dve", lambda e, t=t: e.tensor_scalar(out=ep[:, t, :], in0=o0v[:, t, 0:64],
                                                               scalar1=es[:, t:t + 1], scalar2=None, op0=ALU.mult),
                         reads=rd + [esB], writes=[epB])
                    B.op("dve", lambda e, t=t: e.tensor_scalar(out=ep[:, 4 + t, :], in0=o1v[:, t, 0:64],
                                                               scalar1=es[:, 4 + t:5 + t], scalar2=sm[:, 4:5],
                                                               op0=ALU.mult, op1=ALU.mult),
                         reads=rd + [esB, epB], writes=[epB])
                    B.op("dve", lambda e, t=t: e.tensor_tensor(out=ep[:, t, :], in0=ep[:, t, :], in1=ep[:, 4 + t, :],
                                                               op=ALU.add), reads=[epB], writes=[epB])
                    B.op("dve", lambda e, t=t: e.scalar_tensor_tensor(out=ep[:, 4 + t, :], in0=ep[:, t, :],
                                                                      scalar=1.0 / 64, in1=ep[:, t, :],
                                                                      op0=ALU.mult, op1=ALU.mult,
                                                                      accum_out=es[:, 8 + t:9 + t]),
                         reads=[epB, esB], writes=[epB, esB])
                B.op("act", lambda e: e.activation(out=es[:, 12:16], in_=es[:, 8:12], func=AF.Ln, bias=EPS),
                     reads=[esB], writes=[esB])
                B.op("act", lambda e: e.activation(out=es[:, 12:16], in_=es[:, 12:16], func=AF.Exp, scale=-0.5),
                     reads=[esB], writes=[esB])
                for t in range(4):
                    j = 4 * g + t
                    B.op("dve", lambda e, t=t: e.scalar_tensor_tensor(out=ep[:, t, :], in0=ep[:, t, :],
                                                                      scalar=es[:, 12 + t:13 + t], in1=subg[:, :],
                                                                      op0=ALU.mult, op1=ALU.mult),
                         reads=[epB, esB] + constB, writes=[epB])
                    B.op("dve", lambda e, t=t, j=j: e.tensor_tensor(out=gy[:, j, h * 64:(h + 1) * 64],
                                                                    in0=ep[:, t, :], in1=gy[:, j, h * 64:(h + 1) * 64],
                                                                    op=ALU.mult),
                         reads=[epB, gyB[j]], writes=[gyB[j]])

        def run_B(g2):
            b = load_head("B", g2)
            kt, vt = KT[0], Vt[0]
            qx = KT[1]
            cnt = 0
            for j in range(NSLOT):
                m_, s = divmod(j, 2)
                base = 8 * m_ + 4 * s
                ob = 4 + (j % 2)
                first = True
                steps = [rel for rel in range(5) if base + rel - 1 >= 0]
                for si, rel in enumerate(steps):
                    kb = base + rel - 1
                    pos = pos_of(kb)
                    pb = cnt % 2
                    pti = cnt % 2
                    cnt += 1
                    mi = s * 5 + rel
                    B._deps("pe", [KTB[0], KTB[1]] + constB, [psB[pb]])
                    B.pe_quiet(lambda e, pb=pb, pos=pos, j=j: e.matmul(
                        ps[pb][:, :], lhsT=kt[0:73, pos * 128:(pos + 1) * 128], rhs=qx[0:73, j * 512:(j + 1) * 512],
                        start=True, stop=False))
                    for hh in range(4):
                        fn2 = lambda e, pb=pb, hh=hh, mi=mi: e.matmul(
                            ps[pb][:, hh * 128:(hh + 1) * 128], lhsT=ident[:, :],
                            rhs=maskB[:, mi * 128:(mi + 1) * 128], start=False, stop=(hh == 3))
                        if hh < 3:
                            B.pe_quiet(fn2)
                        else:
                            B.op("pe", fn2, reads=[KTB[0], KTB[1]] + constB, writes=[psB[pb]])
                    B.op("act", lambda e, pb=pb, pti=pti: e.activation(out=PT[pti][:, :], in_=ps[pb][:, :],
                                                                       func=AF.Exp),
                         reads=[psB[pb]], writes=[PTB[pti]])
                    B._deps("pe", [PTB[pti], VtB[b]], [psB[ob]])
                    for hh in range(4):
                        fn = lambda e, pti=pti, ob=ob, hh=hh, pos=pos, st=first, sp_=(si == len(steps) - 1): e.matmul(
                            ps[ob][:, hh * 72:hh * 72 + 65], lhsT=PT[pti][:, hh * 128:(hh + 1) * 128],
                            rhs=vt[:, pos, 0:65], start=st, stop=sp_, skip_group_check=True)
                        first = False
                        if hh < 3:
                            B.pe_quiet(fn)
                        else:
                            B.op("pe", fn, reads=[PTB[pti], VtB[b]], writes=[psB[ob]])
                ov = ps[ob][:, 0:288].rearrange("p (t w) -> p t w", t=4)
                rd = [psB[ob]] + constB
                B.op("dve", lambda e, ov=ov: e.tensor_tensor(out=es[:, 0:4], in0=ov[:, :, 64],
                                                             in1=sm[:, 8 + g2 * 4:12 + g2 * 4], op=ALU.add),
                     reads=rd + [esB], writes=[esB])
                B.op("dve", lambda e: e.reciprocal(out=es[:, 0:4], in_=es[:, 0:4]), reads=[esB], writes=[esB])
                for hh in range(4):
                    c0 = 256 + (g2 * 4 + hh) * 64
                    B.op("dve", lambda e, hh=hh, ov=ov, j=j, c0=c0: e.scalar_tensor_tensor(
                        out=gy[:, j, c0:c0 + 64], in0=ov[:, hh, 0:64], scalar=es[:, hh:hh + 1],
                        in1=gy[:, j, c0:c0 + 64], op0=ALU.mult, op1=ALU.mult),
                        reads=rd + [esB, gyB[j]], writes=[gyB[j]])

        def run_C(hp):
            b = load_head("C", hp)
            kt, qt = KT[b], QT[b]
            for m in range(2):
                hh = hp * 2 + m
                vt = Vt[b]
                vtB_ = VtB[b]
                v0 = m * 64
                p0 = m * 64
                for g in range(4):
                    par = grpc[0] % 2
                    grpc[0] += 1
                    slots = slot_info(g)
                    nk = slots[3][2]
                    ob = 5 + par
                    first_pv = True
                    first_cs = True
                    for kb in range(nk - 1, -1, -1):
                        sbuf = stepc[0] % 2
                        stepc[0] += 1
                        act_slots = [s_ for s_ in slots if kb < s_[2]]
                        t0 = act_slots[0][0]
                        c0 = t0 * 128
                        pos = pos_of(kb)
                        near = [s_ for s_ in act_slots if kb >= s_[2] - 4]
                        newslots = [s_ for s_ in act_slots if kb == s_[2] - 1]
                        zb = sbuf
                        ab = 2 + sbuf
                        kq = [KTB[b], QTB[b]] + constB
                        B.op("pe", lambda e, zb=zb, pos=pos, c0=c0, g=g: e.matmul(
                            ps[zb][:, c0:512], lhsT=kt[p0:p0 + 64, pos * 128:(pos + 1) * 128],
                            rhs=qt[p0:p0 + 64, 4 * g * 128 + c0:(4 * g + 4) * 128], start=True, stop=True),
                            reads=kq, writes=[psB[zb]])
                        B.op("act", lambda e, zb=zb, sbuf=sbuf, c0=c0: e.activation(
                            out=Et[sbuf][:, c0:512], in_=ps[zb][:, c0:512], func=AF.Exp),
                            reads=[psB[zb]], writes=[EtB[sbuf]])
                        B.op("act", lambda e, sbuf=sbuf, c0=c0: e.activation(
                            out=SPt[sbuf][:, c0:512], in_=Et[sbuf][:, c0:512], func=AF.Ln, bias=1.0),
                            reads=[EtB[sbuf]], writes=[SPB[sbuf]])
                        for (t, j, L, s) in near:
                            mi = s * 4 + (kb - (L - 4))
                            B.op("pool", lambda e, sbuf=sbuf, t=t, mi=mi: e.tensor_tensor(
                                out=SPt[sbuf][:, t * 128:(t + 1) * 128], in0=SPt[sbuf][:, t * 128:(t + 1) * 128],
                                in1=m01C[:, mi * 128:(mi + 1) * 128], op=ALU.mult),
                                reads=[SPB[sbuf]] + constB, writes=[SPB[sbuf]])
                        for (t, j, L, s) in newslots:
                            B.op("pool", lambda e, t=t: e.memset(Rt[:, t * 128:(t + 1) * 128], 0.0),
                                 writes=[RB])
                        B._deps("pe", kq + [SPB[sbuf], RB], [psB[ab]])
                        B.pe_quiet(lambda e, ab=ab, pos=pos, c0=c0, g=g: e.matmul(
                            ps[ab][:, c0:512], lhsT=kt[p0:p0 + 64, pos * 128:(pos + 1) * 128],
                            rhs=qt[p0:p0 + 64, 4 * g * 128 + c0:(4 * g + 4) * 128], start=True, stop=False))
                        B.pe_quiet(lambda e, ab=ab, sbuf=sbuf, c0=c0: e.matmul(
                            ps[ab][:, c0:512], lhsT=negU[:, :], rhs=SPt[sbuf][:, c0:512], start=False, stop=False))
                        for (t, j, L, s) in near:
                            mi = s * 4 + (kb - (L - 4))
                            B.pe_quiet(lambda e, ab=ab, t=t, mi=mi: e.matmul(
                                ps[ab][:, t * 128:(t + 1) * 128], lhsT=ident[:, :],
                                rhs=maskC[:, mi * 128:(mi + 1) * 128], start=False, stop=False))
                        B.op("pe", lambda e, ab=ab, c0=c0: e.matmul(
                            ps[ab][:, c0:512], lhsT=negSel[:, :], rhs=Rt[:, c0:512], start=False, stop=True),
                            reads=kq + [SPB[sbuf], RB], writes=[psB[ab]])
                        B.op("pe", lambda e, sbuf=sbuf, c0=c0, st=first_cs, sp_=(kb == 0): e.matmul(
                            ps[4][0:64, c0:512], lhsT=ones64[:, :], rhs=SPt[sbuf][:, c0:512], start=st, stop=sp_,
                            skip_group_check=True),
                            reads=[SPB[sbuf]] + constB, writes=[psB[4]])
                        first_cs = False
                        if kb > 0:
                            B.op("dve", lambda e, c0=c0: e.tensor_copy(out=Rt[:, c0:512], in_=ps[4][0:64, c0:512]),
                                 reads=[psB[4]], writes=[RB])
                            B.op("dve", lambda e, c0=c0: e.tensor_tensor(out=Rt[32:64, c0:512],
                                                                         in0=ps[4][32:64, c0:512],
                                                                         in1=Rt[32:64, c0:512], op=ALU.subtract),
                                 reads=[psB[4], RB], writes=[RB])
                        pti = sbuf
                        B.op("act", lambda e, ab=ab, pti=pti, c0=c0: e.activation(
                            out=PT[pti][:, c0:512], in_=ps[ab][:, c0:512], func=AF.Exp),
                            reads=[psB[ab]], writes=[PTB[pti]])
                        B._deps("pe", [PTB[pti], vtB_], [psB[ob]])
                        for ai, (t, j, L, s) in enumerate(act_slots):
                            fn = lambda e, pti=pti, t=t, pos=pos, st=first_pv, sp_=(kb == 0): e.matmul(
                                ps[ob][:, t * 64:(t + 1) * 64], lhsT=PT[pti][:, t * 128:(t + 1) * 128],
                                rhs=vt[:, pos, v0:v0 + 64], start=st, stop=sp_, skip_group_check=True)
                            first_pv = False
                            if ai < len(act_slots) - 1:
                                B.pe_quiet(fn)
                            else:
                                B.op("pe", fn, reads=[PTB[pti], vtB_], writes=[psB[ob]])
                    for t in range(4):
                        j = 4 * g + t
                        c0 = 768 + hh * 64
                        B.op("dve", lambda e, t=t, j=j, c0=c0, ob=ob: e.tensor_tensor(
                            out=gy[:, j, c0:c0 + 64], in0=ps[ob][:, t * 64:(t + 1) * 64], in1=gy[:, j, c0:c0 + 64],
                            op=ALU.mult), reads=[psB[ob], gyB[j]], writes=[gyB[j]])

        for h in range(4):
            run_A(h)
        for g2 in range(2):
            run_B(g2)
        for hp in range(2):
            run_C(hp)
        B.barrier()
        B.stack.close()
        B.stack = old_stack

    def phase_out(l):
        B.barrier()
        old_stack = B.stack
        B.stack = contextlib.ExitStack()
        wo = B.sb([128, 8, D], BF16)
        woB = Buf()
        wst = [B.sb([128, 8, 256], F32) for _ in range(2)]
        wstB = [Buf(), Buf()]
        wv = wout_d[l, :, :].rearrange("(c p) n -> p c n", p=128)
        for ci in range(4):
            b = ci % 2
            B.dma("sp", wst[b][:, :, :], wv[:, :, ci * 256:(ci + 1) * 256], writes=[wstB[b]])
            B.op("pool", lambda e, b=b, ci=ci: e.tensor_copy(out=wo[:, :, ci * 256:(ci + 1) * 256],
                                                             in_=wst[b][:, :, :]),
                 reads=[wstB[b]], writes=[woB])
        yT = [B.sb([128, 8, 128], BF16) for _ in range(2)]
        yTB = [Buf(), Buf()]
        for j in range(NSLOT):
            b = j % 2
            pb = 6 + b
            tp = ps[pb][:, :].bitcast(BF16)
            B._deps("pe", [gyB[j], identB], [psB[pb]])
            for c in range(8):
                fn = lambda e, c=c, j=j, tp=tp: e.transpose(out=tp[:, c * 128:(c + 1) * 128],
                                                           in_=gy[:, j, c * 128:(c + 1) * 128], identity=ident[:, :])
                if c < 7:
                    B.pe_quiet(fn)
                else:
                    B.op("pe", fn, reads=[gyB[j], identB], writes=[psB[pb]])
            B.op("act", lambda e, b=b, tp=tp: e.copy(out=yT[b][:, :, :], in_=tp.rearrange("p (c t) -> p c t", c=8)),
                 reads=[psB[pb]], writes=[yTB[b]])
            for half in range(2):
                ob = 2 * b + half
                B._deps("pe", [yTB[b], woB], [psB[ob]])
                for c in range(8):
                    fn = lambda e, c=c, b=b, ob=ob, half=half: e.matmul(
                        ps[ob][:, :], lhsT=yT[b][:, c, :], rhs=wo[:, c, half * 512:(half + 1) * 512],
                        start=(c == 0), stop=(c == 7))
                    if c < 7:
                        B.pe_quiet(fn)
                    else:
                        B.op("pe", fn, reads=[yTB[b], woB], writes=[psB[ob]])
                B.op("dve", lambda e, j=j, ob=ob, half=half: e.tensor_tensor(
                    out=x_sb[:, j, half * 512:(half + 1) * 512], in0=ps[ob][:, :],
                    in1=x_sb[:, j, half * 512:(half + 1) * 512], op=ALU.add),
                    reads=[psB[ob], xB[j]], writes=[xB[j]])
        B.barrier()
        B.stack.close()
        B.stack = old_stack

    def phase_final():
        B.barrier()
        gbc = B.sb([128, D], F32)
        gB = Buf()
        B.dma("sp", gbc[:, :], fg_d[:, :], writes=[gB])
        ssq = B.sb([128, NSLOT], F32)
        rstd = B.sb([128, NSLOT], F32)
        junk = B.sb([128, D], F32)
        junkB, ssqB, rstdB = Buf(), Buf(), Buf()
        ot = [B.sb([128, D], F32) for _ in range(2)]
        otB = [Buf(), Buf()]
        for j in range(NSLOT):
            B.op("act", lambda e, j=j: e.activation(out=junk[:, :], in_=x_sb[:, j, :], func=AF.Square,
                                                    accum_out=ssq[:, j:j + 1]),
                 reads=[xB[j]], writes=[junkB, ssqB])
        B.op("act", lambda e: e.activation(out=rstd[:, :], in_=ssq[:, :], func=AF.Ln, scale=1.0 / D, bias=EPS),
             reads=[ssqB], writes=[rstdB])
        B.op("act", lambda e: e.activation(out=rstd[:, :], in_=rstd[:, :], func=AF.Exp, scale=-0.5),
             reads=[rstdB], writes=[rstdB])
        outB = Buf()
        for j in range(NSLOT):
            b = j % 2
            B.op("dve", lambda e, j=j, b=b: e.scalar_tensor_tensor(out=ot[b][:, :], in0=x_sb[:, j, :],
                                                                   scalar=rstd[:, j:j + 1], in1=gbc[:, :],
                                                                   op0=ALU.mult, op1=ALU.mult),
                 reads=[xB[j], rstdB, gB], writes=[otB[b]])
            B.dma("sp", out_d[j, :, :], ot[b][:, :], reads=[otB[b]], writes=[outB])
        B.barrier()

    for l in range(DEPTH):
        if l < nlay_proj:
            phase_proj(l)
            gather(l)
        if l < nlay_attn:
            phase_attn(l)
            phase_out(l)
            if stage == 2 and l == 0:
                dbgB = Buf()
                for j in range(NSLOT):
                    B.dma("sp", xdbg_d[j, :, :], x_sb[:, j, :], reads=[xB[j]], writes=[dbgB], part=True)
    if stage >= 3:
        phase_final()
    else:
        B.barrier()
    return nc


_NC_CACHE = {}


def _get_nc(stage):
    if stage not in _NC_CACHE:
        _NC_CACHE[stage] = build(stage)
    return _NC_CACHE[stage]


def _core_inputs(c, inputs):
    b, r = divmod(c, 4)
    x = np.asarray(inputs["x"], np.float32)
    xs = np.stack([x[b, qb_of(r, j) * 128:(qb_of(r, j) + 1) * 128, :] for j in range(NSLOT)])
    d = {"x": np.ascontiguousarray(xs)}
    d["w_in"] = np.ascontiguousarray(np.asarray(inputs["w_in"], np.float32))
    d["w_out"] = np.ascontiguousarray(np.asarray(inputs["w_out"], np.float32))
    d["ng"] = np.ascontiguousarray(np.broadcast_to(np.asarray(inputs["norm_g"], np.float32)[:, None, :], (DEPTH, 128, D)))
    d["fg"] = np.ascontiguousarray(np.broadcast_to(np.asarray(inputs["final_g"], np.float32)[None, :], (128, D)))
    lam4 = np.concatenate([np.asarray(inputs[k], np.float32) for k in ("lambda_q1", "lambda_k1", "lambda_q2", "lambda_k2")], axis=1)
    d["lam4"] = np.ascontiguousarray(np.broadcast_to(lam4[:, None, :], (DEPTH, 128, 128)))
    d["subg"] = np.ascontiguousarray(np.broadcast_to(np.asarray(inputs["subln_g"], np.float32)[:, None, :], (DEPTH, 128, 64)))
    d["sinks"] = np.ascontiguousarray(np.broadcast_to(np.asarray(inputs["sinks"], np.float32)[:, None, :], (DEPTH, 128, 8)))
    d.update(host_tables(r))
    return d


def _gather_host(res, name):
    out = []
    for c in range(8):
        grp = (c // 4) * 4
        out.append(np.concatenate([np.asarray(res[grp + rr][name]) for rr in range(4)], axis=0))
    return out


def kernel(**inputs):
    ins = [_core_inputs(c, inputs) for c in range(8)]
    cores = list(range(8))
    FUSED = False
    if FUSED:
        res = run_bass_kernel_spmd(_get_nc(4), ins, core_ids=cores).results
    else:
        r1 = run_bass_kernel_spmd(_get_nc(1), ins, core_ids=cores).results
        k0 = _gather_host(r1, "ktl0")
        v0 = _gather_host(r1, "vl0")
        for c in range(8):
            ins[c]["kta0"] = k0[c]
            ins[c]["va0"] = v0[c]
        r2 = run_bass_kernel_spmd(_get_nc(2), ins, core_ids=cores).results
        k1 = _gather_host(r2, "ktl1")
        v1 = _gather_host(r2, "vl1")
        for c in range(8):
            ins[c]["kta1"] = k1[c]
            ins[c]["va1"] = v1[c]
        res = run_bass_kernel_spmd(_get_nc(3), ins, core_ids=cores).results
    out = np.zeros((2, SEQ, D), np.float32)
    for c in range(8):
        b, r = divmod(c, 4)
        o = np.asarray(res[c]["out"])
        for j in range(NSLOT):
            qb = qb_of(r, j)
            out[b, qb * 128:(qb + 1) * 128, :] = o[j]
    return out
```

```python
import math
import contextlib
import numpy as np
import ml_dtypes
import concourse.bass as bass
import concourse.mybir as mybir
from concourse.bass_utils import run_bass_kernel_spmd

F32 = mybir.dt.float32
BF16 = mybir.dt.bfloat16
AF = mybir.ActivationFunctionType
ALU = mybir.AluOpType
AX = mybir.AxisListType
NPBF = ml_dtypes.bfloat16

D = 1024
SEQ = 8192
DEPTH = 2
DIN = 3328
NSLOT = 16
TOK = 2048
EPS = 1e-5
NEG = -30000.0
KTROWS = 640
VW = 688
VA0, VB0, VC0 = 0, 288, 432
AQ, AK, AV, AG = 0, 256, 512, 768
BQ, BK, BV, BG = 1024, 1536, 1664, 1792
CQ, CK, CV, CG = 2304, 2560, 2816, 3072


def pos_of(kb):
    m, q = divmod(kb, 8)
    if q < 4:
        r, j = q, 2 * m
    else:
        r, j = 7 - q, 2 * m + 1
    return r * 16 + j


def qb_of(r, j):
    m, s = divmod(j, 2)
    return 8 * m + (r if s == 0 else 7 - r)


def kb_of_pos(pos):
    r, j = divmod(pos, 16)
    return qb_of(r, j)


def split3(v):
    v = np.asarray(v, np.float64)
    hi = v.astype(NPBF)
    mid = (v - hi.astype(np.float64)).astype(NPBF)
    lo = (v - hi.astype(np.float64) - mid.astype(np.float64)).astype(NPBF)
    return hi, mid, lo


def alibi():
    s = (2.0 ** (-8.0 * np.arange(1, 13) / 12)).astype(np.float32)
    return s[8:].astype(np.float64), s[:8].astype(np.float64)


def host_tables(r):
    sa, sb = alibi()
    t = {}
    kpos = np.zeros(8192)
    for pos in range(64):
        kpos[pos * 128:(pos + 1) * 128] = kb_of_pos(pos) * 128 + np.arange(128)
    qpos = np.zeros(2048)
    for j in range(16):
        qpos[j * 128:(j + 1) * 128] = qb_of(r, j) * 128 + np.arange(128)
    one_k = np.ones(8192, NPBF)
    one_q = np.ones(2048, NPBF)
    ka = np.zeros((4, 6, 8192), NPBF)
    qa = np.zeros((4, 6, 2048), NPBF)
    for h in range(4):
        w = split3(sa[h] * kpos)
        u = split3(-sa[h] * qpos)
        for i in range(3):
            ka[h, i] = one_k
            ka[h, 3 + i] = w[i]
            qa[h, i] = u[i]
            qa[h, 3 + i] = one_q
    t["kaugA"] = ka
    t["qaugA"] = qa
    kbase = (kpos // 128) * 128
    ki = kpos % 128
    kb_ = np.zeros((9, 8192), NPBF)
    kb_[0:3] = 1
    for i in range(3):
        kb_[3 + 2 * i] = kbase.astype(NPBF)
        kb_[4 + 2 * i] = ki.astype(NPBF)
    t["kaugB"] = kb_
    qbt = np.zeros((2, 9, 16, 4, 128), NPBF)
    for g2 in range(2):
        for hh in range(4):
            s = sb[g2 * 4 + hh]
            u = split3((-s * qpos).reshape(16, 128))
            s3 = split3(np.array([s]))
            for i in range(3):
                qbt[g2, i, :, hh, :] = u[i]
                qbt[g2, 3 + 2 * i, :, hh, :] = s3[i][0]
                qbt[g2, 4 + 2 * i, :, hh, :] = s3[i][0]
    t["qaugB"] = qbt.reshape(2, 9, 8192)
    kk = np.arange(128)[:, None]
    qq = np.arange(128)[None, :]
    mA = np.zeros((128, 8, 128), np.float32)
    mC = np.zeros((128, 8, 128), np.float32)
    for s in range(2):
        qr = r if s == 0 else 3 - r
        for rel in range(4):
            if rel == qr:
                mA[:, s * 4 + rel, :] = np.where(kk > qq, NEG, 0.0)
                mC[:, s * 4 + rel, :] = np.where(kk >= qq, NEG, 0.0)
            elif rel > qr:
                mA[:, s * 4 + rel, :] = NEG
                mC[:, s * 4 + rel, :] = NEG
    t["maskA"] = mA.reshape(128, 1024).astype(NPBF)
    t["maskC"] = mC.reshape(128, 1024).astype(NPBF)
    t["m01C"] = (mC == 0).astype(np.float32).reshape(128, 1024).astype(NPBF)
    mB = np.full((128, 10, 128), NEG, np.float32)
    for s in range(2):
        qr = r if s == 0 else 3 - r
        for rel in range(5):
            delta = qr - rel + 1
            if delta == 0:
                mB[:, s * 5 + rel, :] = np.where(kk > qq, NEG, 0.0)
            elif delta == 1:
                mB[:, s * 5 + rel, :] = np.where(kk > qq, 0.0, NEG)
    t["maskB"] = mB.reshape(128, 1280).astype(NPBF)
    t["ident"] = np.eye(128, dtype=np.float32).astype(NPBF)
    t["negU"] = np.where(kk >= qq, -1.0, 0.0).astype(NPBF)
    sel = np.zeros((64, 128), np.float32)
    sel[0, :] = -1.0
    sel[32, :] = -1.0
    t["negSel"] = sel.astype(NPBF)
    t["ones64"] = np.ones((128, 64), NPBF)
    return t


class Buf:
    __slots__ = ("w", "r")

    def __init__(self):
        self.w = {}
        self.r = {}


class Builder:
    def __init__(self):
        self.nc = bass.Bass("TRN2", target_bir_lowering=False)
        nc = self.nc
        self.eng = {"pe": nc.tensor, "act": nc.scalar, "dve": nc.vector, "pool": nc.gpsimd, "sp": nc.sync}
        self.esem = {}
        self.ecnt = {}
        for e in ("pe", "act", "dve", "pool"):
            self.esem[e] = nc.semaphore("es_" + e).__enter__()
            self.ecnt[e] = 0
        self.ND = 48
        self.dsem = [nc.semaphore("ds%d" % i).__enter__() for i in range(self.ND)]
        self.dcnt = [0] * self.ND
        self.dnext = 0
        self.waited = {e: {} for e in self.eng}
        self.nt = 0
        self.stack = contextlib.ExitStack()

    def sb(self, shape, dt, name=None):
        self.nt += 1
        return self.stack.enter_context(self.nc.sbuf_tensor(name or "t%d" % self.nt, list(shape), dt))

    def dram(self, name, shape, dt, kind="Internal"):
        return self.nc.dram_tensor(name, list(shape), dt, kind=kind).ap()

    def _wait(self, e, tok):
        if tok is None:
            return
        name, sem, val, prod = tok
        if prod == "pe" and e == "pe":
            return
        if self.waited[e].get(name, 0) >= val:
            return
        self.eng[e].wait_ge(sem, val)
        self.waited[e][name] = val

    def _deps(self, e, reads, writes, part=False):
        for b in reads:
            for t in b.w.values():
                self._wait(e, t)
        for b in writes:
            for t in b.r.values():
                self._wait(e, t)
            if not part:
                for t in b.w.values():
                    self._wait(e, t)

    def _mark(self, tok, reads, writes, part=False):
        key = tok[3] + tok[0]
        for b in reads:
            b.r[key] = tok
        for b in writes:
            if part:
                b.w[key] = tok
            else:
                b.w = {key: tok}
                b.r = {}

    def op(self, e, fn, reads=(), writes=()):
        self._deps(e, reads, writes)
        ins = fn(self.eng[e])
        self.ecnt[e] += 1
        ins.then_inc(self.esem[e], 1)
        tok = ("es_" + e, self.esem[e], self.ecnt[e], e)
        self._mark(tok, reads, writes)
        return tok

    def pe_quiet(self, fn):
        fn(self.eng["pe"])

    def dma(self, q, out, in_, reads=(), writes=(), part=False):
        self._deps(q, reads, writes, part)
        i = self.dnext
        self.dnext = (i + 1) % self.ND
        name = "ds%d" % i
        if self.dcnt[i] > 0:
            self._wait(q, (name, self.dsem[i], self.dcnt[i], "dma"))
        ins = self.eng[q].dma_start(out=out, in_=in_)
        self.dcnt[i] += 16
        ins.then_inc(self.dsem[i], 16)
        tok = (name, self.dsem[i], self.dcnt[i], "dma")
        self._mark(tok, reads, writes, part)
        return tok

    def barrier(self):
        toks = []
        for e in ("pe", "act", "dve", "pool"):
            if self.ecnt[e] > 0:
                toks.append(("es_" + e, self.esem[e], self.ecnt[e], "x"))
        for i in range(self.ND):
            if self.dcnt[i] > 0:
                toks.append(("ds%d" % i, self.dsem[i], self.dcnt[i], "dma"))
        for e in self.eng:
            for t in toks:
                if t[0] == "es_" + e:
                    continue
                self._wait(e, t)


def build(stage, r_dummy=None):
    B = Builder()
    nc = B.nc
    fused = stage == 4
    nlay_attn = {1: 0, 2: 1, 3: 2, 4: 2}[stage]
    nlay_proj = {1: 1, 2: 2, 3: 2, 4: 2}[stage]

    def ext_in(name, shape, dt):
        return B.dram(name, shape, dt, kind="ExternalInput")

    x_d = ext_in("x", [NSLOT, 128, D], F32)
    win_d = ext_in("w_in", [DEPTH, D, DIN], F32)
    wout_d = ext_in("w_out", [DEPTH, D, D], F32)
    ng_d = ext_in("ng", [DEPTH, 128, D], F32)
    fg_d = ext_in("fg", [128, D], F32)
    lam_d = ext_in("lam4", [DEPTH, 128, 128], F32)
    subg_d = ext_in("subg", [DEPTH, 128, 64], F32)
    sink_d = ext_in("sinks", [DEPTH, 128, 8], F32)
    ident_d = ext_in("ident", [128, 128], BF16)
    kaugA_d = ext_in("kaugA", [4, 6, 8192], BF16)
    qaugA_d = ext_in("qaugA", [4, 6, 2048], BF16)
    kaugB_d = ext_in("kaugB", [9, 8192], BF16)
    qaugB_d = ext_in("qaugB", [2, 9, 8192], BF16)
    maskA_d = ext_in("maskA", [128, 1024], BF16)
    maskC_d = ext_in("maskC", [128, 1024], BF16)
    m01C_d = ext_in("m01C", [128, 1024], BF16)
    maskB_d = ext_in("maskB", [128, 1280], BF16)
    negU_d = ext_in("negU", [128, 128], BF16)
    negSel_d = ext_in("negSel", [64, 128], BF16)
    ones64_d = ext_in("ones64", [128, 64], BF16)

    ktl_d, vl_d, kta_d, va_d = [], [], [], []
    kta_full, va_full = [], []
    for l in range(DEPTH):
        last_proj = (l == nlay_proj - 1) and stage in (1, 2)
        ktl_d.append(B.dram("ktl%d" % l, [KTROWS, TOK], BF16, kind="ExternalOutput" if last_proj else "Internal"))
        vl_d.append(B.dram("vl%d" % l, [TOK, VW], BF16, kind="ExternalOutput" if last_proj else "Internal"))
        if fused:
            ktf = B.dram("kta%d" % l, [8 * KTROWS, TOK], BF16)
            vaf = B.dram("va%d" % l, [8 * TOK, VW], BF16)
            kta_full.append(ktf)
            va_full.append(vaf)
            kta_d.append(B.dram("ktm%d" % l, [4 * KTROWS, TOK], BF16))
            va_d.append(B.dram("vam%d" % l, [4 * TOK, VW], BF16))
        elif l < nlay_attn:
            kta_d.append(ext_in("kta%d" % l, [4 * KTROWS, TOK], BF16))
            va_d.append(ext_in("va%d" % l, [4 * TOK, VW], BF16))
    ql_d = B.dram("ql", [1024, TOK], BF16, kind="ExternalOutput" if stage == 1 else "Internal")
    xdbg_d = B.dram("xdbg", [NSLOT, 128, D], F32, kind="ExternalOutput") if stage == 2 else None
    out_d = None
    if stage >= 3:
        out_d = B.dram("out", [NSLOT, 128, D], F32, kind="ExternalOutput")

    x_sb = B.sb([128, NSLOT, D], F32, "x_sb")
    gy = B.sb([128, NSLOT, D], BF16, "gy")
    ident = B.sb([128, 128], BF16, "ident_sb")
    xB = [Buf() for _ in range(NSLOT)]
    gyB = [Buf() for _ in range(NSLOT)]
    identB = Buf()
    ps = [nc.psum_tensor("ps%d" % i, [128, 512], F32).__enter__() for i in range(8)]
    psB = [Buf() for _ in range(8)]
    qlB = Buf()
    ktlB = [Buf() for _ in range(DEPTH)]
    vlB = [Buf() for _ in range(DEPTH)]
    ktaB = [Buf() for _ in range(DEPTH)]
    vaB = [Buf() for _ in range(DEPTH)]

    B.dma("sp", ident[:, :], ident_d[:, :], writes=[identB])
    for j in range(NSLOT):
        B.dma("sp", x_sb[:, j, :], x_d[j, :, :], writes=[xB[j]])

    sa, sb_sl = alibi()

    def rms_to_hT(l, hT, hTB, gsrc):
        gbc = B.sb([128, D], F32)
        gB = Buf()
        B.dma("sp", gbc[:, :], gsrc, writes=[gB])
        ssq = B.sb([128, NSLOT], F32)
        rstd = B.sb([128, NSLOT], F32)
        junk = B.sb([128, D], F32)
        junkB = Buf()
        ssqB = Buf()
        rstdB = Buf()
        hb = [B.sb([128, D], BF16) for _ in range(2)]
        hbB = [Buf(), Buf()]
        for j in range(NSLOT):
            B.op("act", lambda e, j=j: e.activation(out=junk[:, :], in_=x_sb[:, j, :], func=AF.Square,
                                                    accum_out=ssq[:, j:j + 1]),
                 reads=[xB[j]], writes=[junkB, ssqB])
        B.op("act", lambda e: e.activation(out=rstd[:, :], in_=ssq[:, :], func=AF.Ln, scale=1.0 / D, bias=EPS),
             reads=[ssqB], writes=[rstdB])
        B.op("act", lambda e: e.activation(out=rstd[:, :], in_=rstd[:, :], func=AF.Exp, scale=-0.5),
             reads=[rstdB], writes=[rstdB])
        for j in range(NSLOT):
            b = j % 2
            B.op("dve", lambda e, j=j, b=b: e.scalar_tensor_tensor(out=hb[b][:, :], in0=x_sb[:, j, :],
                                                                   scalar=rstd[:, j:j + 1], in1=gbc[:, :],
                                                                   op0=ALU.mult, op1=ALU.mult),
                 reads=[xB[j], rstdB, gB], writes=[hbB[b]])
            pb = 6 + b
            tp = ps[pb][:, :].bitcast(BF16)
            for c in range(8):
                fn = lambda e, c=c, b=b, tp=tp: e.transpose(out=tp[:, c * 128:(c + 1) * 128],
                                                           in_=hb[b][:, c * 128:(c + 1) * 128],
                                                           identity=ident[:, :])
                if c == 0:
                    B._deps("pe", [hbB[b], identB], [psB[pb]])
                if c < 7:
                    B.pe_quiet(fn)
                else:
                    B.op("pe", fn, reads=[hbB[b], identB], writes=[psB[pb]])
            eng = "act" if j % 2 == 0 else "dve"
            if eng == "act":
                B.op("act", lambda e, j=j, tp=tp: e.copy(out=hT[:, :, j * 128:(j + 1) * 128],
                                                         in_=tp.rearrange("p (c t) -> p c t", c=8)),
                     reads=[psB[pb]], writes=[hTB])
            else:
                B.op("dve", lambda e, j=j, tp=tp: e.tensor_copy(out=hT[:, :, j * 128:(j + 1) * 128],
                                                                in_=tp.rearrange("p (c t) -> p c t", c=8)),
                     reads=[psB[pb]], writes=[hTB])

    def phase_proj(l):
        B.barrier()
        old_stack = B.stack
        B.stack = contextlib.ExitStack()
        hT = B.sb([128, 8, TOK], BF16)
        hTB = Buf()
        rms_to_hT(l, hT, hTB, ng_d[l, :, :])
        wst = [B.sb([128, 8, 256], F32) for _ in range(2)]
        wbf = [B.sb([128, 8, 256], BF16) for _ in range(2)]
        wstB = [Buf(), Buf()]
        wbfB = [Buf(), Buf()]
        fst = [B.sb([128, TOK], BF16) for _ in range(2)]
        fstB = [Buf(), Buf()]
        vst = B.sb([128, NSLOT, 288], BF16)
        vstB = Buf()
        win_v = win_d[l, :, :].rearrange("(c p) n -> p c n", p=128)
        chunks = []
        chunks.append((AQ, [("F", "q", 0, 32 ** -0.5), ("F", "q", 128, 32 ** -0.5)]))
        chunks.append((AK, [("F", "k", 0, 1.0), ("F", "k", 128, 1.0)]))
        chunks.append((AV, [("V", VA0, 4, 72, 256)]))
        chunks.append((AG, [("G", 0, 256)]))
        chunks.append((BQ, [("F", "q", 256, 0.125), ("F", "q", 384, 0.125)]))
        chunks.append((BQ + 256, [("F", "q", 512, 0.125), ("F", "q", 640, 0.125)]))
        chunks.append((BK, [("F", "k", 256, 1.0), ("V1", VB0, 2, 72, 128)]))
        chunks.append((BG, [("G", 256, 256)]))
        chunks.append((BG + 256, [("G", 512, 256)]))
        chunks.append((CQ, [("F", "q", 768, 0.125), ("F", "q", 896, 0.125)]))
        chunks.append((CK, [("F", "k", 384, 1.0), ("F", "k", 512, 1.0)]))
        chunks.append((CV, [("V", VC0, 4, 64, 256)]))
        chunks.append((CG, [("G", 768, 256)]))
        fcount = 0
        pcount = 0
        vinit = False
        for ci, (col0, kinds) in enumerate(chunks):
            b = ci % 2
            B.dma("sp", wst[b][:, :, :], win_v[:, :, col0:col0 + 256], writes=[wstB[b]])
            B.op("pool", lambda e, b=b: e.tensor_copy(out=wbf[b][:, :, :], in_=wst[b][:, :, :]),
                 reads=[wstB[b]], writes=[wbfB[b]])
            for hi, kd in enumerate(kinds):
                if kd[0] == "F":
                    _, dst, row0, scale = kd
                    c0 = hi * 128
                    fb = fcount % 2
                    fcount += 1
                    for tg in range(4):
                        pb = pcount % 4
                        pcount += 1
                        B._deps("pe", [wbfB[b], hTB], [psB[pb]])
                        for c in range(8):
                            fn = lambda e, c=c, pb=pb, b=b, c0=c0, tg=tg: e.matmul(
                                ps[pb][:, :], lhsT=wbf[b][:, c, c0:c0 + 128], rhs=hT[:, c, tg * 512:(tg + 1) * 512],
                                start=(c == 0), stop=(c == 7))
                            if c < 7:
                                B.pe_quiet(fn)
                            else:
                                B.op("pe", fn, reads=[wbfB[b], hTB], writes=[psB[pb]])
                        B.op("dve", lambda e, pb=pb, fb=fb, tg=tg, scale=scale: e.tensor_scalar(
                            out=fst[fb][:, tg * 512:(tg + 1) * 512], in0=ps[pb][:, :], scalar1=float(scale),
                            scalar2=None, op0=ALU.mult),
                            reads=[psB[pb]], writes=[fstB[fb]])
                    if dst == "q":
                        B.dma("pool", ql_d[row0:row0 + 128, :], fst[fb][:, :], reads=[fstB[fb]], writes=[qlB])
                    else:
                        B.dma("pool", ktl_d[l][row0:row0 + 128, :], fst[fb][:, :], reads=[fstB[fb]],
                              writes=[ktlB[l]])
                elif kd[0] in ("V", "V1"):
                    _, vcol0, nh, hw, width = kd
                    c0 = 128 if kd[0] == "V1" else 0
                    if hw == 72:
                        B.op("pool", lambda e, nh=nh: e.memset(vst[:, :, 0:nh * 72], 1.0), writes=[vstB])
                    for j in range(NSLOT):
                        pb = pcount % 4
                        pcount += 1
                        B._deps("pe", [wbfB[b], hTB], [psB[pb]])
                        for c in range(8):
                            fn = lambda e, c=c, pb=pb, b=b, c0=c0, j=j, width=width: e.matmul(
                                ps[pb][:, 0:width], lhsT=hT[:, c, j * 128:(j + 1) * 128],
                                rhs=wbf[b][:, c, c0:c0 + width], start=(c == 0), stop=(c == 7))
                            if c < 7:
                                B.pe_quiet(fn)
                            else:
                                B.op("pe", fn, reads=[wbfB[b], hTB], writes=[psB[pb]])
                        B.op("dve", lambda e, pb=pb, j=j, nh=nh, hw=hw, width=width: e.tensor_copy(
                            out=vst[:, j, 0:nh * hw].rearrange("p (h w) -> p h w", h=nh)[:, :, 0:64],
                            in_=ps[pb][:, 0:width].rearrange("p (h w) -> p h w", h=nh)),
                            reads=[psB[pb]], writes=[vstB])
                    B.dma("pool", vl_d[l][:, vcol0:vcol0 + nh * hw].rearrange("(j p) w -> p j w", p=128),
                          vst[:, :, 0:nh * hw], reads=[vstB], writes=[vlB[l]])
                else:
                    _, gcol0, width = kd
                    for j in range(NSLOT):
                        pb = pcount % 4
                        pcount += 1
                        B._deps("pe", [wbfB[b], hTB], [psB[pb]])
                        for c in range(8):
                            fn = lambda e, c=c, pb=pb, b=b, j=j, width=width: e.matmul(
                                ps[pb][:, 0:width], lhsT=hT[:, c, j * 128:(j + 1) * 128],
                                rhs=wbf[b][:, c, 0:width], start=(c == 0), stop=(c == 7))
                            if c < 7:
                                B.pe_quiet(fn)
                            else:
                                B.op("pe", fn, reads=[wbfB[b], hTB], writes=[psB[pb]])
                        B.op("act", lambda e, pb=pb, j=j, gcol0=gcol0, width=width: e.activation(
                            out=gy[:, j, gcol0:gcol0 + width], in_=ps[pb][:, 0:width], func=AF.Silu),
                            reads=[psB[pb]], writes=[gyB[j]])
        B.barrier()
        B.stack.close()
        B.stack = old_stack

    def gather(l):
        if not fused:
            return
        B._deps("pool", [ktlB[l], vlB[l]], [ktaB[l], vaB[l]])
        for src, dst in ((ktl_d[l], kta_full[l]), (vl_d[l], va_full[l])):
            ins = nc.gpsimd.collective_compute("AllGather", ALU.bypass,
                                               replica_groups=[[0, 1, 2, 3, 4, 5, 6, 7]],
                                               ins=[src[:, :]], outs=[dst[:, :]])
            B.ecnt["pool"] += 1
            ins.then_inc(B.esem["pool"], 1)
        tok = ("es_pool", B.esem["pool"], B.ecnt["pool"], "pool")
        fullB = Buf()
        B._mark(tok, [ktlB[l], vlB[l]], [fullB])
        bidx = nc.sync.partition_id() // 4
        ksrc = kta_full[l].rearrange("(b x) n -> b x n", b=2)[bass.ds(bidx, 1), :, :][0]
        vsrc = va_full[l].rearrange("(b x) n -> b x n", b=2)[bass.ds(bidx, 1), :, :][0]
        for p_ in range(4):
            B.dma("sp", kta_d[l][p_ * KTROWS:(p_ + 1) * KTROWS, :], ksrc[p_ * KTROWS:(p_ + 1) * KTROWS, :],
                  reads=[fullB], writes=[ktaB[l]], part=True)
            B.dma("sp", va_d[l][p_ * TOK:(p_ + 1) * TOK, :], vsrc[p_ * TOK:(p_ + 1) * TOK, :],
                  reads=[fullB], writes=[vaB[l]], part=True)

    def slot_info(g):
        res = []
        for t in range(4):
            j = 4 * g + t
            m, s = divmod(j, 2)
            L = 8 * m + 4 + 4 * s
            res.append((t, j, L, s))
        return res

    def phase_attn(l):
        B.barrier()
        old_stack = B.stack
        B.stack = contextlib.ExitStack()
        lambda_init = 0.8 - 0.6 * math.exp(-0.3 * l)
        maskA = B.sb([128, 1024], BF16)
        maskC = B.sb([128, 1024], BF16)
        m01C = B.sb([128, 1024], BF16)
        maskB = B.sb([128, 1280], BF16)
        negU = B.sb([128, 128], BF16)
        negSel = B.sb([64, 128], BF16)
        ones64 = B.sb([128, 64], BF16)
        cB = Buf()
        for dst, src in ((maskA, maskA_d), (maskC, maskC_d), (m01C, m01C_d), (maskB, maskB_d), (negU, negU_d),
                         (negSel, negSel_d), (ones64, ones64_d)):
            B.dma("sp", dst[:, :], src[:, :], writes=[cB], part=True)
        lamt = B.sb([128, 128], F32)
        subg = B.sb([128, 64], F32)
        sinkt = B.sb([128, 8], F32)
        B.dma("sp", lamt[:, :], lam_d[l, :, :], writes=[cB], part=True)
        B.dma("sp", subg[:, :], subg_d[l, :, :], writes=[cB], part=True)
        B.dma("sp", sinkt[:, :], sink_d[l, :, :], writes=[cB], part=True)
        sm = B.sb([128, 16], F32)
        smB = Buf()
        prod = B.sb([128, 64], F32)
        B.op("dve", lambda e: e.tensor_tensor(out=prod[:, 0:32], in0=lamt[:, 0:32], in1=lamt[:, 32:64], op=ALU.mult),
             reads=[cB], writes=[smB])
        B.op("dve", lambda e: e.tensor_tensor(out=prod[:, 32:64], in0=lamt[:, 64:96], in1=lamt[:, 96:128],
                                              op=ALU.mult), reads=[cB, smB], writes=[smB])
        B.op("dve", lambda e: e.reduce_sum(out=sm[:, 0:1], in_=prod[:, 0:32], axis=AX.X), reads=[smB], writes=[smB])
        B.op("dve", lambda e: e.reduce_sum(out=sm[:, 1:2], in_=prod[:, 32:64], axis=AX.X), reads=[smB], writes=[smB])
        B.op("act", lambda e: e.activation(out=sm[:, 2:4], in_=sm[:, 0:2], func=AF.Exp), reads=[smB], writes=[smB])
        B.op("dve", lambda e: e.tensor_tensor(out=sm[:, 4:5], in0=sm[:, 3:4], in1=sm[:, 2:3], op=ALU.subtract),
             reads=[smB], writes=[smB])
        B.op("dve", lambda e: e.tensor_scalar(out=sm[:, 4:5], in0=sm[:, 4:5], scalar1=-float(lambda_init),
                                              scalar2=None, op0=ALU.add), reads=[smB], writes=[smB])
        B.op("dve", lambda e: e.tensor_scalar(out=subg[:, :], in0=subg[:, :], scalar1=float(1.0 - lambda_init),
                                              scalar2=None, op0=ALU.mult), reads=[cB, smB], writes=[smB])
        B.op("act", lambda e: e.activation(out=sm[:, 8:16], in_=sinkt[:, :], func=AF.Exp), reads=[cB, smB],
             writes=[smB])
        constB = [cB, smB, identB]

        kta = kta_d[l]
        va = va_d[l]
        KT = [B.sb([128, 8192], BF16) for _ in range(2)]
        KTB = [Buf(), Buf()]
        QT = [B.sb([128, TOK], BF16) for _ in range(2)]
        QT1 = [B.sb([128, TOK], BF16) for _ in range(2)]
        QTB = [Buf(), Buf()]
        for b_ in range(2):
            B.op("pool", lambda e, b_=b_: e.memset(KT[b_][:, :], 0.0), writes=[KTB[b_]])
            B.op("pool", lambda e, b_=b_: e.memset(QT[b_][:, :], 0.0), writes=[QTB[b_]])
            B.op("pool", lambda e, b_=b_: e.memset(QT1[b_][:, :], 0.0), writes=[QTB[b_]])
        Vt = [B.sb([128, 64, 128], BF16) for _ in range(2)]
        VtB = [Buf(), Buf()]
        PT = [B.sb([128, 512], BF16) for _ in range(4)]
        PTB = [Buf() for _ in range(4)]
        Et = [B.sb([128, 512], F32) for _ in range(2)]
        EtB = [Buf(), Buf()]
        SPt = [B.sb([128, 512], BF16) for _ in range(2)]
        SPB = [Buf(), Buf()]
        Rt = B.sb([64, 512], BF16)
        RB = Buf()
        ep = B.sb([128, 8, 64], F32)
        epB = Buf()
        es = B.sb([128, 16], F32)
        esB = Buf()
        hcount = [0]

        def load_head(kind, h):
            if kind == "B":
                b = 0
            else:
                b = hcount[0] % 2
                hcount[0] += 1
            kt, qt, vt = KT[b], QT[b], Vt[b]
            ktv = kta.rearrange("(r w) n -> w r n", r=4)
            vav = va.rearrange("(r j p) w -> p (r j) w", r=4, p=128)
            ktd = lambda rows: kt[rows, :].rearrange("p (r n) -> p r n", r=4)
            if kind == "A":
                for m in range(2):
                    r0 = (h * 2 + m) * 32
                    for rr in range(4):
                        B.dma("sp", kt[m * 64:m * 64 + 32, rr * TOK:(rr + 1) * TOK],
                              kta[rr * KTROWS + r0:rr * KTROWS + r0 + 32, :], reads=[ktaB[l]],
                              writes=[KTB[b]], part=True)
                    B.dma("sp", kt[m * 64 + 32:m * 64 + 38, :], kaugA_d[h, :, :], writes=[KTB[b]], part=True)
                    qtm = qt if m == 0 else QT1[b]
                    B.dma("sp", qtm[m * 64:m * 64 + 32, :], ql_d[r0:r0 + 32, :], reads=[qlB], writes=[QTB[b]], part=True)
                    B.dma("sp", qtm[m * 64 + 32:m * 64 + 38, :], qaugA_d[h, :, :], writes=[QTB[b]], part=True)
                B.dma("sp", vt[:, :, 0:72], vav[:, :, VA0 + h * 72:VA0 + (h + 1) * 72], reads=[vaB[l]],
                      writes=[VtB[b]], part=True)
            elif kind == "C":
                for m in range(2):
                    hh = h * 2 + m
                    for rr in range(4):
                        B.dma("sp", kt[m * 64:m * 64 + 64, rr * TOK:(rr + 1) * TOK],
                              kta[rr * KTROWS + 384 + hh * 64:rr * KTROWS + 384 + (hh + 1) * 64, :],
                              reads=[ktaB[l]], writes=[KTB[b]], part=True)
                    qtm = qt if m == 0 else QT1[b]
                    B.dma("sp", qtm[m * 64:m * 64 + 64, :], ql_d[768 + hh * 64:768 + (hh + 1) * 64, :], reads=[qlB],
                          writes=[QTB[b]], part=True)
                for m in range(2):
                    B.dma("sp", vt[:, :, m * 64:(m + 1) * 64],
                          vav[:, :, VC0 + h * 128 + m * 64:VC0 + h * 128 + (m + 1) * 64], reads=[vaB[l]],
                          writes=[VtB[b]], part=True)
            else:
                qx = KT[1]
                for rr in range(4):
                    B.dma("sp", kt[0:64, rr * TOK:(rr + 1) * TOK],
                          kta[rr * KTROWS + 256 + h * 64:rr * KTROWS + 256 + (h + 1) * 64, :], reads=[ktaB[l]],
                          writes=[KTB[0]], part=True)
                B.dma("sp", kt[64:73, :], kaugB_d[:, :], writes=[KTB[0]], part=True)
                for hh in range(4):
                    r0 = 256 + (h * 4 + hh) * 64
                    B.dma("sp", qx[0:64, :].rearrange("p (j hh i) -> p j hh i", j=16, hh=4)[:, :, hh, :],
                          ql_d[r0:r0 + 64, :].rearrange("p (j i) -> p j i", j=16), reads=[qlB], writes=[KTB[1]], part=True)
                B.dma("sp", qx[64:73, :], qaugB_d[h, :, :], writes=[KTB[1]], part=True)
                B.dma("sp", vt[:, :, 0:72], vav[:, :, VB0 + h * 72:VB0 + (h + 1) * 72], reads=[vaB[l]],
                      writes=[VtB[0]], part=True)
            return b

        mhalf = B.sb([128, 4], F32)
        B.op("pool", lambda e: e.memset(mhalf[:, :], -0.5), writes=[cB])
        Rt2 = [Rt, B.sb([64, 512], BF16)]
        RB2 = [RB, Buf()]
        hb = {}

        def emit_A():
            units = []
            for h in range(4):
                for g in range(4):
                    slots = slot_info(g)
                    nk = slots[3][2]
                    for kb in range(nk):
                        for m in range(2):
                            units.append((h, g, kb, m, slots, nk))
            first_pv = {}

            def ensure(h):
                if h < 4 and ("A", h) not in hb:
                    hb[("A", h)] = load_head("A", h)

            def qk_exp(u, i):
                h, g, kb, m, slots, nk = u
                b = hb[("A", h)]
                kt, qt = KT[b], QT[b]
                act_slots = [s_ for s_ in slots if kb < s_[2]]
                t0 = act_slots[0][0]
                pos = pos_of(kb)
                pb = i % 4
                near = [s_ for s_ in act_slots if kb >= s_[2] - 4]
                rds = [KTB[b], QTB[b]] + constB
                B._deps("pe", rds, [psB[pb]])
                fn = lambda e: e.matmul(
                    ps[pb][:, t0 * 128:512], lhsT=kt[:, pos * 128:(pos + 1) * 128],
                    rhs=(qt if m == 0 else QT1[b])[:, (4 * g + t0) * 128:(4 * g + 4) * 128], start=True,
                    stop=(len(near) == 0))
                if near:
                    B.pe_quiet(fn)
                    for ni, (t, j, L, s) in enumerate(near):
                        mi = s * 4 + (kb - (L - 4))
                        fn2 = lambda e, t=t, mi=mi, last=(ni == len(near) - 1): e.matmul(
                            ps[pb][:, t * 128:(t + 1) * 128], lhsT=ident[:, :],
                            rhs=maskA[:, mi * 128:(mi + 1) * 128], start=False, stop=last)
                        if ni < len(near) - 1:
                            B.pe_quiet(fn2)
                        else:
                            B.op("pe", fn2, reads=rds, writes=[psB[pb]])
                else:
                    B.op("pe", fn, reads=rds, writes=[psB[pb]])
                B.op("act", lambda e: e.activation(out=PT[pb][:, t0 * 128:512], in_=ps[pb][:, t0 * 128:512],
                                                   func=AF.Exp), reads=[psB[pb]], writes=[PTB[pb]])

            def pv(u, i):
                h, g, kb, m, slots, nk = u
                b = hb[("A", h)]
                vt = Vt[b]
                act_slots = [s_ for s_ in slots if kb < s_[2]]
                pos = pos_of(kb)
                pti = i % 4
                par = (h * 4 + g) % 2
                ob = 4 + 2 * par + m
                B._deps("pe", [PTB[pti], VtB[b]], [psB[ob]])
                for ai, (t, j, L, s) in enumerate(act_slots):
                    st = first_pv.get((h, g, m), True)
                    first_pv[(h, g, m)] = False
                    fn = lambda e, t=t, st=st, sp_=(kb == L - 1): e.matmul(
                        ps[ob][:, t * 72:t * 72 + 65], lhsT=PT[pti][:, t * 128:(t + 1) * 128],
                        rhs=vt[:, pos, 0:65], start=st, stop=sp_, skip_group_check=True)
                    if ai < len(act_slots) - 1:
                        B.pe_quiet(fn)
                    else:
                        B.op("pe", fn, reads=[PTB[pti], VtB[b]], writes=[psB[ob]])

            def epilogue(u):
                h, g, kb, m, slots, nk = u
                par = (h * 4 + g) % 2
                Ob = [4 + 2 * par, 5 + 2 * par]
                O0, O1 = ps[Ob[0]], ps[Ob[1]]
                rd = [psB[Ob[0]], psB[Ob[1]]] + constB
                o0v = O0[:, 0:288].rearrange("p (t w) -> p t w", t=4)
                o1v = O1[:, 0:288].rearrange("p (t w) -> p t w", t=4)
                B.op("dve", lambda e: e.reciprocal(out=es[:, 0:4], in_=o0v[:, :, 64]), reads=rd, writes=[esB])
                B.op("dve", lambda e: e.reciprocal(out=es[:, 4:8], in_=o1v[:, :, 64]), reads=rd + [esB], writes=[esB])
                for t in range(4):
                    B.op("dve", lambda e, t=t: e.tensor_scalar(out=ep[:, t, :], in0=o0v[:, t, 0:64],
                                                               scalar1=es[:, t:t + 1], scalar2=None, op0=ALU.mult),
                         reads=rd + [esB], writes=[epB])
                    B.op("dve", lambda e, t=t: e.tensor_scalar(out=ep[:, 4 + t, :], in0=o1v[:, t, 0:64],
                                                               scalar1=es[:, 4 + t:5 + t], scalar2=sm[:, 4:5],
                                                               op0=ALU.mult, op1=ALU.mult),
                         reads=rd + [esB, epB], writes=[epB])
                    B.op("dve", lambda e, t=t: e.tensor_tensor(out=ep[:, t, :], in0=ep[:, t, :], in1=ep[:, 4 + t, :],
                                                               op=ALU.add), reads=[epB], writes=[epB])
                    B.op("dve", lambda e, t=t: e.scalar_tensor_tensor(out=ep[:, 4 + t, :], in0=ep[:, t, :],
                                                                      scalar=1.0 / 64, in1=ep[:, t, :],
                                                                      op0=ALU.mult, op1=ALU.mult,
                                                                      accum_out=es[:, 8 + t:9 + t]),
                         reads=[epB, esB], writes=[epB, esB])
                B.op("dve", lambda e: e.tensor_scalar(out=es[:, 8:12], in0=es[:, 8:12], scalar1=EPS, scalar2=None,
                                                      op0=ALU.add), reads=[esB], writes=[esB])
                B.op("pool", lambda e: e.tensor_tensor(out=es[:, 12:16], in0=es[:, 8:12], in1=mhalf[:, :],
                                                       op=ALU.pow), reads=[esB, cB], writes=[esB])
                for t in range(4):
                    j = 4 * g + t
                    B.op("dve", lambda e, t=t: e.scalar_tensor_tensor(out=ep[:, t, :], in0=ep[:, t, :],
                                                                      scalar=es[:, 12 + t:13 + t], in1=subg[:, :],
                                                                      op0=ALU.mult, op1=ALU.mult),
                         reads=[epB, esB] + constB, writes=[epB])
                    B.op("dve", lambda e, t=t, j=j: e.tensor_tensor(out=gy[:, j, h * 64:(h + 1) * 64],
                                                                    in0=ep[:, t, :], in1=gy[:, j, h * 64:(h + 1) * 64],
                                                                    op=ALU.mult),
                         reads=[epB, gyB[j]], writes=[gyB[j]])

            ensure(0)
            DD = 2
            n = len(units)
            for i in range(n + DD):
                if i < n:
                    u = units[i]
                    if u[1] == 0 and u[2] == 0 and u[3] == 0:
                        ensure(u[0])
                    if u[1] == 1 and u[2] == 0 and u[3] == 0:
                        ensure(u[0] + 1)
                    qk_exp(u, i)
                if i >= DD:
                    u = units[i - DD]
                    pv(u, i - DD)
                    if u[2] == u[5] - 1 and u[3] == 1:
                        epilogue(u)

        def emit_B(g2):
            load_head("B", g2)
            kt, vt = KT[0], Vt[0]
            qx = KT[1]
            units = []
            for j in range(NSLOT):
                m_, s = divmod(j, 2)
                base = 8 * m_ + 4 * s
                steps = [rel for rel in range(5) if base + rel - 1 >= 0]
                for si, rel in enumerate(steps):
                    units.append((j, s, base + rel - 1, rel, si == 0, si == len(steps) - 1))

            def qk_exp(u, i):
                j, s, kb, rel, first, last = u
                pos = pos_of(kb)
                pb = i % 4
                mi = s * 5 + rel
                rds = [KTB[0], KTB[1]] + constB
                B._deps("pe", rds, [psB[pb]])
                B.pe_quiet(lambda e: e.matmul(
                    ps[pb][:, :], lhsT=kt[0:73, pos * 128:(pos + 1) * 128], rhs=qx[0:73, j * 512:(j + 1) * 512],
                    start=True, stop=False))
                for hh in range(4):
                    fn2 = lambda e, hh=hh: e.matmul(
                        ps[pb][:, hh * 128:(hh + 1) * 128], lhsT=ident[:, :],
                        rhs=maskB[:, mi * 128:(mi + 1) * 128], start=False, stop=(hh == 3))
                    if hh < 3:
                        B.pe_quiet(fn2)
                    else:
                        B.op("pe", fn2, reads=rds, writes=[psB[pb]])
                B.op("act", lambda e: e.activation(out=PT[pb][:, :], in_=ps[pb][:, :], func=AF.Exp),
                     reads=[psB[pb]], writes=[PTB[pb]])

            def pv(u, i):
                j, s, kb, rel, first, last = u
                pos = pos_of(kb)
                pti = i % 4
                ob = 4 + (j % 2)
                B._deps("pe", [PTB[pti], VtB[0]], [psB[ob]])
                for hh in range(4):
                    fn = lambda e, hh=hh, st=(first and hh == 0): e.matmul(
                        ps[ob][:, hh * 72:hh * 72 + 65], lhsT=PT[pti][:, hh * 128:(hh + 1) * 128],
                        rhs=vt[:, pos, 0:65], start=st, stop=last, skip_group_check=True)
                    if hh < 3:
                        B.pe_quiet(fn)
                    else:
                        B.op("pe", fn, reads=[PTB[pti], VtB[0]], writes=[psB[ob]])
                if last:
                    ov = ps[ob][:, 0:288].rearrange("p (t w) -> p t w", t=4)
                    rd = [psB[ob]] + constB
                    B.op("dve", lambda e: e.tensor_tensor(out=es[:, 0:4], in0=ov[:, :, 64],
                                                          in1=sm[:, 8 + g2 * 4:12 + g2 * 4], op=ALU.add),
                         reads=rd + [esB], writes=[esB])
                    B.op("dve", lambda e: e.reciprocal(out=es[:, 0:4], in_=es[:, 0:4]), reads=[esB], writes=[esB])
                    for hh in range(4):
                        c0 = 256 + (g2 * 4 + hh) * 64
                        B.op("dve", lambda e, hh=hh, c0=c0: e.scalar_tensor_tensor(
                            out=gy[:, j, c0:c0 + 64], in0=ov[:, hh, 0:64], scalar=es[:, hh:hh + 1],
                            in1=gy[:, j, c0:c0 + 64], op0=ALU.mult, op1=ALU.mult),
                            reads=rd + [esB, gyB[j]], writes=[gyB[j]])

            DD = 2
            n = len(units)
            for i in range(n + DD):
                if i < n:
                    qk_exp(units[i], i)
                if i >= DD:
                    pv(units[i - DD], i - DD)

        def emit_C():
            units = []
            gi = 0
            for hp in range(2):
                for m in range(2):
                    for g in range(4):
                        slots = slot_info(g)
                        nk = slots[3][2]
                        for kb in range(nk - 1, -1, -1):
                            units.append((hp, m, g, kb, slots, nk, gi))
                        gi += 1

            def ensure(hp):
                if hp < 2 and ("C", hp) not in hb:
                    hb[("C", hp)] = load_head("C", hp)

            def geom(u):
                hp, m, g, kb, slots, nk, gi = u
                act_slots = [s_ for s_ in slots if kb < s_[2]]
                t0 = act_slots[0][0]
                near = [s_ for s_ in act_slots if kb >= s_[2] - 4]
                newslots = [s_ for s_ in act_slots if kb == s_[2] - 1]
                return act_slots, t0 * 128, near, newslots

            def stA(u, i):
                hp, m, g, kb, slots, nk, gi = u
                b = hb[("C", hp)]
                kt, qt = KT[b], QT[b]
                p0 = m * 64
                act_slots, c0, near, newslots = geom(u)
                pos = pos_of(kb)
                zb = i % 2
                kq = [KTB[b], QTB[b]] + constB
                B.op("pe", lambda e: e.matmul(
                    ps[zb][:, c0:512], lhsT=kt[:, pos * 128:(pos + 1) * 128],
                    rhs=(qt if m == 0 else QT1[b])[:, 4 * g * 128 + c0:(4 * g + 4) * 128], start=True, stop=True),
                    reads=kq, writes=[psB[zb]])
                B.op("act", lambda e: e.activation(out=Et[zb][:, c0:512], in_=ps[zb][:, c0:512], func=AF.Exp),
                     reads=[psB[zb]], writes=[EtB[zb]])
                B.op("act", lambda e: e.activation(out=SPt[zb][:, c0:512], in_=Et[zb][:, c0:512], func=AF.Ln,
                                                   bias=1.0), reads=[EtB[zb]], writes=[SPB[zb]])
                for (t, j, L, s) in near:
                    mi = s * 4 + (kb - (L - 4))
                    B.op("pool", lambda e, t=t, mi=mi: e.tensor_tensor(
                        out=SPt[zb][:, t * 128:(t + 1) * 128], in0=SPt[zb][:, t * 128:(t + 1) * 128],
                        in1=m01C[:, mi * 128:(mi + 1) * 128], op=ALU.mult),
                        reads=[SPB[zb]] + constB, writes=[SPB[zb]])

            def stB(u, i):
                hp, m, g, kb, slots, nk, gi = u
                b = hb[("C", hp)]
                kt, qt = KT[b], QT[b]
                p0 = m * 64
                act_slots, c0, near, newslots = geom(u)
                pos = pos_of(kb)
                zb = i % 2
                ab = 2 + zb
                first = (kb == nk - 1)
                kq = [KTB[b], QTB[b]] + constB
                rprev, rprevB = Rt2[(i + 1) % 2], RB2[(i + 1) % 2]
                rcur, rcurB = Rt2[i % 2], RB2[i % 2]
                B.op("pe", lambda e: e.matmul(
                    ps[4][0:64, c0:512], lhsT=ones64[:, :], rhs=SPt[zb][:, c0:512], start=first, stop=(kb == 0),
                    skip_group_check=True), reads=[SPB[zb]] + constB, writes=[psB[4]])
                if kb > 0:
                    B.op("dve", lambda e: e.tensor_copy(out=rcur[:, c0:512], in_=ps[4][0:64, c0:512]),
                         reads=[psB[4]], writes=[rcurB])
                    B.op("dve", lambda e: e.tensor_tensor(out=rcur[32:64, c0:512], in0=ps[4][32:64, c0:512],
                                                          in1=rcur[32:64, c0:512], op=ALU.subtract),
                         reads=[psB[4], rcurB], writes=[rcurB])
                for (t, j, L, s) in newslots:
                    B.op("pool", lambda e, t=t: e.memset(rprev[:, t * 128:(t + 1) * 128], 0.0), writes=[rprevB])
                B._deps("pe", kq + [SPB[zb], rprevB], [psB[ab]])
                B.pe_quiet(lambda e: e.matmul(
                    ps[ab][:, c0:512], lhsT=kt[:, pos * 128:(pos + 1) * 128],
                    rhs=(qt if m == 0 else QT1[b])[:, 4 * g * 128 + c0:(4 * g + 4) * 128], start=True, stop=False))
                B.pe_quiet(lambda e: e.matmul(
                    ps[ab][:, c0:512], lhsT=negU[:, :], rhs=SPt[zb][:, c0:512], start=False, stop=False))
                for (t, j, L, s) in near:
                    mi = s * 4 + (kb - (L - 4))
                    B.pe_quiet(lambda e, t=t, mi=mi: e.matmul(
                        ps[ab][:, t * 128:(t + 1) * 128], lhsT=ident[:, :],
                        rhs=maskC[:, mi * 128:(mi + 1) * 128], start=False, stop=False))
                B.op("pe", lambda e: e.matmul(
                    ps[ab][:, c0:512], lhsT=negSel[:, :], rhs=rprev[:, c0:512], start=False, stop=True),
                    reads=kq + [SPB[zb], rprevB], writes=[psB[ab]])
                B.op("act", lambda e: e.activation(out=PT[zb][:, c0:512], in_=ps[ab][:, c0:512], func=AF.Exp),
                     reads=[psB[ab]], writes=[PTB[zb]])

            def stC(u, i):
                hp, m, g, kb, slots, nk, gi = u
                b = hb[("C", hp)]
                vt = Vt[b]
                v0 = m * 64
                hh = hp * 2 + m
                act_slots, c0, near, newslots = geom(u)
                pos = pos_of(kb)
                pti = i % 2
                ob = 5 + gi % 2
                first = (kb == nk - 1)
                B._deps("pe", [PTB[pti], VtB[b]], [psB[ob]])
                for ai, (t, j, L, s) in enumerate(act_slots):
                    fn = lambda e, t=t, st=(first and ai == 0): e.matmul(
                        ps[ob][:, t * 64:(t + 1) * 64], lhsT=PT[pti][:, t * 128:(t + 1) * 128],
                        rhs=vt[:, pos, v0:v0 + 64], start=st, stop=(kb == 0), skip_group_check=True)
                    if ai < len(act_slots) - 1:
                        B.pe_quiet(fn)
                    else:
                        B.op("pe", fn, reads=[PTB[pti], VtB[b]], writes=[psB[ob]])
                if kb == 0:
                    for t in range(4):
                        j = 4 * g + t
                        cc = 768 + hh * 64
                        B.op("dve", lambda e, t=t, j=j, cc=cc: e.tensor_tensor(
                            out=gy[:, j, cc:cc + 64], in0=ps[ob][:, t * 64:(t + 1) * 64], in1=gy[:, j, cc:cc + 64],
                            op=ALU.mult), reads=[psB[ob], gyB[j]], writes=[gyB[j]])

            ensure(0)
            n = len(units)
            for i in range(n + 2):
                if i < n:
                    u = units[i]
                    if u[1] == 0 and u[2] == 0 and u[3] == u[5] - 1:
                        ensure(u[0])
                    if u[1] == 0 and u[2] == 1 and u[3] == u[5] - 1:
                        ensure(u[0] + 1)
                    stA(u, i)
                if 1 <= i <= n:
                    stB(units[i - 1], i - 1)
                if i >= 2:
                    stC(units[i - 2], i - 2)

        emit_A()
        emit_B(0)
        emit_B(1)
        hcount[0] = 0
        emit_C()
        B.barrier()
        B.stack.close()
        B.stack = old_stack

    def phase_out(l):
        B.barrier()
        old_stack = B.stack
        B.stack = contextlib.ExitStack()
        wo = B.sb([128, 8, D], BF16)
        woB = Buf()
        wst = [B.sb([128, 8, 256], F32) for _ in range(2)]
        wstB = [Buf(), Buf()]
        wv = wout_d[l, :, :].rearrange("(c p) n -> p c n", p=128)
        for ci in range(4):
            b = ci % 2
            B.dma("sp", wst[b][:, :, :], wv[:, :, ci * 256:(ci + 1) * 256], writes=[wstB[b]])
            B.op("pool", lambda e, b=b, ci=ci: e.tensor_copy(out=wo[:, :, ci * 256:(ci + 1) * 256],
                                                             in_=wst[b][:, :, :]),
                 reads=[wstB[b]], writes=[woB])
        yT = [B.sb([128, 8, 128], BF16) for _ in range(2)]
        yTB = [Buf(), Buf()]
        for j in range(NSLOT):
            b = j % 2
            pb = 6 + b
            tp = ps[pb][:, :].bitcast(BF16)
            B._deps("pe", [gyB[j], identB], [psB[pb]])
            for c in range(8):
                fn = lambda e, c=c, j=j, tp=tp: e.transpose(out=tp[:, c * 128:(c + 1) * 128],
                                                           in_=gy[:, j, c * 128:(c + 1) * 128], identity=ident[:, :])
                if c < 7:
                    B.pe_quiet(fn)
                else:
                    B.op("pe", fn, reads=[gyB[j], identB], writes=[psB[pb]])
            B.op("act", lambda e, b=b, tp=tp: e.copy(out=yT[b][:, :, :], in_=tp.rearrange("p (c t) -> p c t", c=8)),
                 reads=[psB[pb]], writes=[yTB[b]])
            for half in range(2):
                ob = 2 * b + half
                B._deps("pe", [yTB[b], woB], [psB[ob]])
                for c in range(8):
                    fn = lambda e, c=c, b=b, ob=ob, half=half: e.matmul(
                        ps[ob][:, :], lhsT=yT[b][:, c, :], rhs=wo[:, c, half * 512:(half + 1) * 512],
                        start=(c == 0), stop=(c == 7))
                    if c < 7:
                        B.pe_quiet(fn)
                    else:
                        B.op("pe", fn, reads=[yTB[b], woB], writes=[psB[ob]])
                B.op("dve", lambda e, j=j, ob=ob, half=half: e.tensor_tensor(
                    out=x_sb[:, j, half * 512:(half + 1) * 512], in0=ps[ob][:, :],
                    in1=x_sb[:, j, half * 512:(half + 1) * 512], op=ALU.add),
                    reads=[psB[ob], xB[j]], writes=[xB[j]])
        B.barrier()
        B.stack.close()
        B.stack = old_stack

    def phase_final():
        B.barrier()
        gbc = B.sb([128, D], F32)
        gB = Buf()
        B.dma("sp", gbc[:, :], fg_d[:, :], writes=[gB])
        ssq = B.sb([128, NSLOT], F32)
        rstd = B.sb([128, NSLOT], F32)
        junk = B.sb([128, D], F32)
        junkB, ssqB, rstdB = Buf(), Buf(), Buf()
        ot = [B.sb([128, D], F32) for _ in range(2)]
        otB = [Buf(), Buf()]
        for j in range(NSLOT):
            B.op("act", lambda e, j=j: e.activation(out=junk[:, :], in_=x_sb[:, j, :], func=AF.Square,
                                                    accum_out=ssq[:, j:j + 1]),
                 reads=[xB[j]], writes=[junkB, ssqB])
        B.op("act", lambda e: e.activation(out=rstd[:, :], in_=ssq[:, :], func=AF.Ln, scale=1.0 / D, bias=EPS),
             reads=[ssqB], writes=[rstdB])
        B.op("act", lambda e: e.activation(out=rstd[:, :], in_=rstd[:, :], func=AF.Exp, scale=-0.5),
             reads=[rstdB], writes=[rstdB])
        outB = Buf()
        for j in range(NSLOT):
            b = j % 2
            B.op("dve", lambda e, j=j, b=b: e.scalar_tensor_tensor(out=ot[b][:, :], in0=x_sb[:, j, :],
                                                                   scalar=rstd[:, j:j + 1], in1=gbc[:, :],
                                                                   op0=ALU.mult, op1=ALU.mult),
                 reads=[xB[j], rstdB, gB], writes=[otB[b]])
            B.dma("sp", out_d[j, :, :], ot[b][:, :], reads=[otB[b]], writes=[outB])
        B.barrier()

    for l in range(DEPTH):
        if l < nlay_proj:
            phase_proj(l)
            gather(l)
        if l < nlay_attn:
            phase_attn(l)
            phase_out(l)
            if stage == 2 and l == 0:
                dbgB = Buf()
                for j in range(NSLOT):
                    B.dma("sp", xdbg_d[j, :, :], x_sb[:, j, :], reads=[xB[j]], writes=[dbgB], part=True)
    if stage >= 3:
        phase_final()
    else:
        B.barrier()
    return nc


_NC_CACHE = {}


def _get_nc(stage):
    if stage not in _NC_CACHE:
        _NC_CACHE[stage] = build(stage)
    return _NC_CACHE[stage]


def _core_inputs(c, inputs):
    b, r = divmod(c, 4)
    x = np.asarray(inputs["x"], np.float32)
    xs = np.stack([x[b, qb_of(r, j) * 128:(qb_of(r, j) + 1) * 128, :] for j in range(NSLOT)])
    d = {"x": np.ascontiguousarray(xs)}
    d["w_in"] = np.ascontiguousarray(np.asarray(inputs["w_in"], np.float32))
    d["w_out"] = np.ascontiguousarray(np.asarray(inputs["w_out"], np.float32))
    d["ng"] = np.ascontiguousarray(np.broadcast_to(np.asarray(inputs["norm_g"], np.float32)[:, None, :], (DEPTH, 128, D)))
    d["fg"] = np.ascontiguousarray(np.broadcast_to(np.asarray(inputs["final_g"], np.float32)[None, :], (128, D)))
    lam4 = np.concatenate([np.asarray(inputs[k], np.float32) for k in ("lambda_q1", "lambda_k1", "lambda_q2", "lambda_k2")], axis=1)
    d["lam4"] = np.ascontiguousarray(np.broadcast_to(lam4[:, None, :], (DEPTH, 128, 128)))
    d["subg"] = np.ascontiguousarray(np.broadcast_to(np.asarray(inputs["subln_g"], np.float32)[:, None, :], (DEPTH, 128, 64)))
    d["sinks"] = np.ascontiguousarray(np.broadcast_to(np.asarray(inputs["sinks"], np.float32)[:, None, :], (DEPTH, 128, 8)))
    d.update(host_tables(r))
    return d


def _gather_host(res, name):
    out = []
    for c in range(8):
        grp = (c // 4) * 4
        out.append(np.concatenate([np.asarray(res[grp + rr][name]) for rr in range(4)], axis=0))
    return out


def kernel(**inputs):
    ins = [_core_inputs(c, inputs) for c in range(8)]
    cores = list(range(8))
    FUSED = True
    if FUSED:
        res = run_bass_kernel_spmd(_get_nc(4), ins, core_ids=cores).results
    else:
        r1 = run_bass_kernel_spmd(_get_nc(1), ins, core_ids=cores).results
        k0 = _gather_host(r1, "ktl0")
        v0 = _gather_host(r1, "vl0")
        for c in range(8):
            ins[c]["kta0"] = k0[c]
            ins[c]["va0"] = v0[c]
        r2 = run_bass_kernel_spmd(_get_nc(2), ins, core_ids=cores).results
        k1 = _gather_host(r2, "ktl1")
        v1 = _gather_host(r2, "vl1")
        for c in range(8):
            ins[c]["kta1"] = k1[c]
            ins[c]["va1"] = v1[c]
        res = run_bass_kernel_spmd(_get_nc(3), ins, core_ids=cores).results
    out = np.zeros((2, SEQ, D), np.float32)
    for c in range(8):
        b, r = divmod(c, 4)
        o = np.asarray(res[c]["out"])
        for j in range(NSLOT):
            qb = qb_of(r, j)
            out[b, qb * 128:(qb + 1) * 128, :] = o[j]
    return out
```

```python
import math
import contextlib
import numpy as np
import ml_dtypes
import concourse.bass as bass
import concourse.mybir as mybir
from concourse.bass_utils import run_bass_kernel_spmd

F32 = mybir.dt.float32
BF16 = mybir.dt.bfloat16
AF = mybir.ActivationFunctionType
ALU = mybir.AluOpType
AX = mybir.AxisListType
NPBF = ml_dtypes.bfloat16

D = 1024
SEQ = 8192
DEPTH = 2
DIN = 3328
NSLOT = 16
TOK = 2048
EPS = 1e-5
NEG = -30000.0
KTROWS = 640
VW = 688
VA0, VB0, VC0 = 0, 288, 432
AQ, AK, AV, AG = 0, 256, 512, 768
BQ, BK, BV, BG = 1024, 1536, 1664, 1792
CQ, CK, CV, CG = 2304, 2560, 2816, 3072


def pos_of(kb):
    m, q = divmod(kb, 8)
    if q < 4:
        r, j = q, 2 * m
    else:
        r, j = 7 - q, 2 * m + 1
    return r * 16 + j


def qb_of(r, j):
    m, s = divmod(j, 2)
    return 8 * m + (r if s == 0 else 7 - r)


def kb_of_pos(pos):
    r, j = divmod(pos, 16)
    return qb_of(r, j)


def split3(v):
    v = np.asarray(v, np.float64)
    hi = v.astype(NPBF)
    mid = (v - hi.astype(np.float64)).astype(NPBF)
    lo = (v - hi.astype(np.float64) - mid.astype(np.float64)).astype(NPBF)
    return hi, mid, lo


def alibi():
    s = (2.0 ** (-8.0 * np.arange(1, 13) / 12)).astype(np.float32)
    return s[8:].astype(np.float64), s[:8].astype(np.float64)


def host_tables(r):
    sa, sb = alibi()
    t = {}
    kpos = np.zeros(8192)
    for pos in range(64):
        kpos[pos * 128:(pos + 1) * 128] = kb_of_pos(pos) * 128 + np.arange(128)
    qpos = np.zeros(2048)
    for j in range(16):
        qpos[j * 128:(j + 1) * 128] = qb_of(r, j) * 128 + np.arange(128)
    one_k = np.ones(8192, NPBF)
    one_q = np.ones(2048, NPBF)
    ka = np.zeros((4, 6, 8192), NPBF)
    qa = np.zeros((4, 6, 2048), NPBF)
    for h in range(4):
        w = split3(sa[h] * kpos)
        u = split3(-sa[h] * qpos)
        for i in range(3):
            ka[h, i] = one_k
            ka[h, 3 + i] = w[i]
            qa[h, i] = u[i]
            qa[h, 3 + i] = one_q
    t["kaugA"] = ka
    t["qaugA"] = qa
    kbase = (kpos // 128) * 128
    ki = kpos % 128
    kb_ = np.zeros((9, 8192), NPBF)
    kb_[0:3] = 1
    for i in range(3):
        kb_[3 + 2 * i] = kbase.astype(NPBF)
        kb_[4 + 2 * i] = ki.astype(NPBF)
    t["kaugB"] = kb_
    qbt = np.zeros((2, 9, 16, 4, 128), NPBF)
    for g2 in range(2):
        for hh in range(4):
            s = sb[g2 * 4 + hh]
            u = split3((-s * qpos).reshape(16, 128))
            s3 = split3(np.array([s]))
            for i in range(3):
                qbt[g2, i, :, hh, :] = u[i]
                qbt[g2, 3 + 2 * i, :, hh, :] = s3[i][0]
                qbt[g2, 4 + 2 * i, :, hh, :] = s3[i][0]
    t["qaugB"] = qbt.reshape(2, 9, 8192)
    kk = np.arange(128)[:, None]
    qq = np.arange(128)[None, :]
    mA = np.zeros((128, 8, 128), np.float32)
    mC = np.zeros((128, 8, 128), np.float32)
    for s in range(2):
        qr = r if s == 0 else 3 - r
        for rel in range(4):
            if rel == qr:
                mA[:, s * 4 + rel, :] = np.where(kk > qq, NEG, 0.0)
                mC[:, s * 4 + rel, :] = np.where(kk >= qq, NEG, 0.0)
            elif rel > qr:
                mA[:, s * 4 + rel, :] = NEG
                mC[:, s * 4 + rel, :] = NEG
    t["maskA"] = mA.reshape(128, 1024).astype(NPBF)
    t["maskC"] = mC.reshape(128, 1024).astype(NPBF)
    t["m01C"] = (mC == 0).astype(np.float32).reshape(128, 1024).astype(NPBF)
    mB = np.full((128, 10, 128), NEG, np.float32)
    for s in range(2):
        qr = r if s == 0 else 3 - r
        for rel in range(5):
            delta = qr - rel + 1
            if delta == 0:
                mB[:, s * 5 + rel, :] = np.where(kk > qq, NEG, 0.0)
            elif delta == 1:
                mB[:, s * 5 + rel, :] = np.where(kk > qq, 0.0, NEG)
    t["maskB"] = mB.reshape(128, 1280).astype(NPBF)
    t["ident"] = np.eye(128, dtype=np.float32).astype(NPBF)
    t["negU"] = np.where(kk >= qq, -1.0, 0.0).astype(NPBF)
    sel = np.zeros((64, 128), np.float32)
    sel[0, :] = -1.0
    sel[32, :] = -1.0
    t["negSel"] = sel.astype(NPBF)
    t["ones64"] = np.ones((128, 64), NPBF)
    return t


class Buf:
    __slots__ = ("w", "r")

    def __init__(self):
        self.w = {}
        self.r = {}


class Builder:
    def __init__(self):
        self.nc = bass.Bass("TRN2", target_bir_lowering=False)
        nc = self.nc
        self.eng = {"pe": nc.tensor, "act": nc.scalar, "dve": nc.vector, "pool": nc.gpsimd, "sp": nc.sync}
        self.esem = {}
        self.ecnt = {}
        for e in ("pe", "act", "dve", "pool"):
            self.esem[e] = nc.semaphore("es_" + e).__enter__()
            self.ecnt[e] = 0
        self.ND = 48
        self.dsem = [nc.semaphore("ds%d" % i).__enter__() for i in range(self.ND)]
        self.dcnt = [0] * self.ND
        self.dnext = 0
        self.waited = {e: {} for e in self.eng}
        self.nt = 0
        self.ccsem = nc.semaphore("ccsem").__enter__()
        self.cccnt = 0
        self.stack = contextlib.ExitStack()

    def sb(self, shape, dt, name=None):
        self.nt += 1
        return self.stack.enter_context(self.nc.sbuf_tensor(name or "t%d" % self.nt, list(shape), dt))

    def dram(self, name, shape, dt, kind="Internal"):
        return self.nc.dram_tensor(name, list(shape), dt, kind=kind).ap()

    def _wait(self, e, tok):
        if tok is None:
            return
        name, sem, val, prod = tok
        if prod == "pe" and e == "pe":
            return
        if self.waited[e].get(name, 0) >= val:
            return
        self.eng[e].wait_ge(sem, val)
        self.waited[e][name] = val

    def _deps(self, e, reads, writes, part=False):
        for b in reads:
            for t in b.w.values():
                self._wait(e, t)
        for b in writes:
            for t in b.r.values():
                self._wait(e, t)
            if not part:
                for t in b.w.values():
                    self._wait(e, t)

    def _mark(self, tok, reads, writes, part=False):
        key = tok[3] + tok[0]
        for b in reads:
            b.r[key] = tok
        for b in writes:
            if part:
                b.w[key] = tok
            else:
                b.w = {key: tok}
                b.r = {}

    def op(self, e, fn, reads=(), writes=()):
        self._deps(e, reads, writes)
        ins = fn(self.eng[e])
        self.ecnt[e] += 1
        ins.then_inc(self.esem[e], 1)
        tok = ("es_" + e, self.esem[e], self.ecnt[e], e)
        self._mark(tok, reads, writes)
        return tok

    def pe_quiet(self, fn):
        fn(self.eng["pe"])

    def dma(self, q, out, in_, reads=(), writes=(), part=False):
        self._deps(q, reads, writes, part)
        i = self.dnext
        self.dnext = (i + 1) % self.ND
        name = "ds%d" % i
        if self.dcnt[i] > 0:
            self._wait(q, (name, self.dsem[i], self.dcnt[i], "dma"))
        ins = self.eng[q].dma_start(out=out, in_=in_)
        self.dcnt[i] += 16
        ins.then_inc(self.dsem[i], 16)
        tok = (name, self.dsem[i], self.dcnt[i], "dma")
        self._mark(tok, reads, writes, part)
        return tok

    def barrier(self):
        toks = []
        for e in ("pe", "act", "dve", "pool"):
            if self.ecnt[e] > 0:
                toks.append(("es_" + e, self.esem[e], self.ecnt[e], "x"))
        for i in range(self.ND):
            if self.dcnt[i] > 0:
                toks.append(("ds%d" % i, self.dsem[i], self.dcnt[i], "dma"))
        for e in self.eng:
            for t in toks:
                if t[0] == "es_" + e:
                    continue
                self._wait(e, t)


def build(stage, r_dummy=None):
    B = Builder()
    nc = B.nc
    fused = stage == 4
    nlay_attn = {1: 0, 2: 1, 3: 2, 4: 2}[stage]
    nlay_proj = {1: 1, 2: 2, 3: 2, 4: 2}[stage]

    def ext_in(name, shape, dt):
        return B.dram(name, shape, dt, kind="ExternalInput")

    x_d = ext_in("x", [NSLOT, 128, D], F32)
    win_d = ext_in("w_in", [DEPTH, D, DIN], F32)
    wout_d = ext_in("w_out", [DEPTH, D, D], F32)
    ng_d = ext_in("ng", [DEPTH, 128, D], F32)
    fg_d = ext_in("fg", [128, D], F32)
    lam_d = ext_in("lam4", [DEPTH, 128, 128], F32)
    subg_d = ext_in("subg", [DEPTH, 128, 64], F32)
    sink_d = ext_in("sinks", [DEPTH, 128, 8], F32)
    ident_d = ext_in("ident", [128, 128], BF16)
    kaugA_d = ext_in("kaugA", [4, 6, 8192], BF16)
    qaugA_d = ext_in("qaugA", [4, 6, 2048], BF16)
    kaugB_d = ext_in("kaugB", [9, 8192], BF16)
    qaugB_d = ext_in("qaugB", [2, 9, 8192], BF16)
    maskA_d = ext_in("maskA", [128, 1024], BF16)
    maskC_d = ext_in("maskC", [128, 1024], BF16)
    m01C_d = ext_in("m01C", [128, 1024], BF16)
    maskB_d = ext_in("maskB", [128, 1280], BF16)
    negU_d = ext_in("negU", [128, 128], BF16)
    negSel_d = ext_in("negSel", [64, 128], BF16)
    ones64_d = ext_in("ones64", [128, 64], BF16)

    assert stage == 4, "only the fused build is supported"
    PARTS = {"A": (256, 288), "BC": (384, 400)}
    kvl_d, kvf_d, kvm_d = [], [], []
    for l in range(DEPTH):
        dl, df, dm = {}, {}, {}
        for pn, (kr, vw) in PARTS.items():
            rows = kr + vw
            dl[pn] = B.dram("kvl%s%d" % (pn, l), [rows, TOK], BF16)
            df[pn] = B.dram("kvf%s%d" % (pn, l), [8 * rows, TOK], BF16)
            dm[pn] = B.dram("kvm%s%d" % (pn, l), [4 * rows, TOK], BF16)
        kvl_d.append(dl)
        kvf_d.append(df)
        kvm_d.append(dm)

    def kview(t, pn, rr=None):
        kr, vw = PARTS[pn]
        base = 0 if rr is None else rr * (kr + vw)
        return t[base:base + kr, :]

    def vview(t, pn, rr=None):
        kr, vw = PARTS[pn]
        base = 0 if rr is None else rr * (kr + vw)
        return t[base + kr:base + kr + vw, :].rearrange("a b -> (a b)").rearrange("(t w) -> t w", w=vw)

    ql_d = B.dram("ql", [1024, TOK], BF16, kind="ExternalOutput" if stage == 1 else "Internal")
    xdbg_d = B.dram("xdbg", [NSLOT, 128, D], F32, kind="ExternalOutput") if stage == 2 else None
    out_d = None
    if stage >= 3:
        out_d = B.dram("out", [NSLOT, 128, D], F32, kind="ExternalOutput")

    x_sb = B.sb([128, NSLOT, D], F32, "x_sb")
    gy = B.sb([128, NSLOT, D], BF16, "gy")
    ident = B.sb([128, 128], BF16, "ident_sb")
    xB = [Buf() for _ in range(NSLOT)]
    gyB = [Buf() for _ in range(NSLOT)]
    identB = Buf()
    ps = [nc.psum_tensor("ps%d" % i, [128, 512], F32).__enter__() for i in range(8)]
    psB = [Buf() for _ in range(8)]
    qlB = Buf()
    kvlB = [{"A": Buf(), "BC": Buf()} for _ in range(DEPTH)]
    kvfB = [{"A": Buf(), "BC": Buf()} for _ in range(DEPTH)]
    kvmB = [{"A": Buf(), "BC": Buf()} for _ in range(DEPTH)]

    B.dma("sp", ident[:, :], ident_d[:, :], writes=[identB])
    for j in range(NSLOT):
        B.dma("sp", x_sb[:, j, :], x_d[j, :, :], writes=[xB[j]])

    sa, sb_sl = alibi()

    def rms_to_hT(l, hT, hTB, gsrc):
        gbc = B.sb([128, D], F32)
        gB = Buf()
        B.dma("sp", gbc[:, :], gsrc, writes=[gB])
        ssq = B.sb([128, NSLOT], F32)
        rstd = B.sb([128, NSLOT], F32)
        junk = B.sb([128, D], F32)
        junkB = Buf()
        ssqB = Buf()
        rstdB = Buf()
        hb = [B.sb([128, D], BF16) for _ in range(2)]
        hbB = [Buf(), Buf()]
        for j in range(NSLOT):
            B.op("act", lambda e, j=j: e.activation(out=junk[:, :], in_=x_sb[:, j, :], func=AF.Square,
                                                    accum_out=ssq[:, j:j + 1]),
                 reads=[xB[j]], writes=[junkB, ssqB])
        B.op("act", lambda e: e.activation(out=rstd[:, :], in_=ssq[:, :], func=AF.Ln, scale=1.0 / D, bias=EPS),
             reads=[ssqB], writes=[rstdB])
        B.op("act", lambda e: e.activation(out=rstd[:, :], in_=rstd[:, :], func=AF.Exp, scale=-0.5),
             reads=[rstdB], writes=[rstdB])
        for j in range(NSLOT):
            b = j % 2
            B.op("dve", lambda e, j=j, b=b: e.scalar_tensor_tensor(out=hb[b][:, :], in0=x_sb[:, j, :],
                                                                   scalar=rstd[:, j:j + 1], in1=gbc[:, :],
                                                                   op0=ALU.mult, op1=ALU.mult),
                 reads=[xB[j], rstdB, gB], writes=[hbB[b]])
            pb = 6 + b
            tp = ps[pb][:, :].bitcast(BF16)
            for c in range(8):
                fn = lambda e, c=c, b=b, tp=tp: e.transpose(out=tp[:, c * 128:(c + 1) * 128],
                                                           in_=hb[b][:, c * 128:(c + 1) * 128],
                                                           identity=ident[:, :])
                if c == 0:
                    B._deps("pe", [hbB[b], identB], [psB[pb]])
                if c < 7:
                    B.pe_quiet(fn)
                else:
                    B.op("pe", fn, reads=[hbB[b], identB], writes=[psB[pb]])
            eng = "act" if j % 2 == 0 else "dve"
            if eng == "act":
                B.op("act", lambda e, j=j, tp=tp: e.copy(out=hT[:, :, j * 128:(j + 1) * 128],
                                                         in_=tp.rearrange("p (c t) -> p c t", c=8)),
                     reads=[psB[pb]], writes=[hTB])
            else:
                B.op("dve", lambda e, j=j, tp=tp: e.tensor_copy(out=hT[:, :, j * 128:(j + 1) * 128],
                                                                in_=tp.rearrange("p (c t) -> p c t", c=8)),
                     reads=[psB[pb]], writes=[hTB])

    def phase_proj(l):
        B.barrier()
        old_stack = B.stack
        B.stack = contextlib.ExitStack()
        hT = B.sb([128, 8, TOK], BF16)
        hTB = Buf()
        rms_to_hT(l, hT, hTB, ng_d[l, :, :])
        wst = [B.sb([128, 8, 256], F32) for _ in range(2)]
        wbf = [B.sb([128, 8, 256], BF16) for _ in range(2)]
        wstB = [Buf(), Buf()]
        wbfB = [Buf(), Buf()]
        fst = [B.sb([128, TOK], BF16) for _ in range(2)]
        fstB = [Buf(), Buf()]
        vst = B.sb([128, NSLOT, 288], BF16)
        vstB = Buf()
        win_v = win_d[l, :, :].rearrange("(c p) n -> p c n", p=128)
        chunks = []
        chunks.append((AK, [("F", "kA", 0, 1.0), ("F", "kA", 128, 1.0)], None))
        chunks.append((AV, [("V", "A", 0, 4, 72, 256)], "A"))
        chunks.append((BK, [("F", "kBC", 0, 1.0), ("V1", "BC", 0, 2, 72, 128)], None))
        chunks.append((CK, [("F", "kBC", 128, 1.0), ("F", "kBC", 256, 1.0)], None))
        chunks.append((CV, [("V", "BC", 144, 4, 64, 256)], "BC"))
        chunks.append((AQ, [("F", "q", 0, 32 ** -0.5), ("F", "q", 128, 32 ** -0.5)], None))
        chunks.append((AG, [("G", 0, 256)], None))
        chunks.append((BQ, [("F", "q", 256, 0.125), ("F", "q", 384, 0.125)], None))
        chunks.append((BQ + 256, [("F", "q", 512, 0.125), ("F", "q", 640, 0.125)], None))
        chunks.append((BG, [("G", 256, 256)], None))
        chunks.append((BG + 256, [("G", 512, 256)], None))
        chunks.append((CQ, [("F", "q", 768, 0.125), ("F", "q", 896, 0.125)], None))
        chunks.append((CG, [("G", 768, 256)], None))
        fcount = 0
        pcount = 0
        vinit = False
        for ci, (col0, kinds, gather_after) in enumerate(chunks):
            b = ci % 2
            B.dma("sp", wst[b][:, :, :], win_v[:, :, col0:col0 + 256], writes=[wstB[b]])
            B.op("act", lambda e, b=b: e.copy(out=wbf[b][:, :, :], in_=wst[b][:, :, :]),
                 reads=[wstB[b]], writes=[wbfB[b]])
            for hi, kd in enumerate(kinds):
                if kd[0] == "F":
                    _, dst, row0, scale = kd
                    c0 = hi * 128
                    fb = fcount % 2
                    fcount += 1
                    for tg in range(4):
                        pb = pcount % 4
                        pcount += 1
                        B._deps("pe", [wbfB[b], hTB], [psB[pb]])
                        for c in range(8):
                            fn = lambda e, c=c, pb=pb, b=b, c0=c0, tg=tg: e.matmul(
                                ps[pb][:, :], lhsT=wbf[b][:, c, c0:c0 + 128], rhs=hT[:, c, tg * 512:(tg + 1) * 512],
                                start=(c == 0), stop=(c == 7))
                            if c < 7:
                                B.pe_quiet(fn)
                            else:
                                B.op("pe", fn, reads=[wbfB[b], hTB], writes=[psB[pb]])
                        B.op("dve", lambda e, pb=pb, fb=fb, tg=tg, scale=scale: e.tensor_scalar(
                            out=fst[fb][:, tg * 512:(tg + 1) * 512], in0=ps[pb][:, :], scalar1=float(scale),
                            scalar2=None, op0=ALU.mult),
                            reads=[psB[pb]], writes=[fstB[fb]])
                    if dst == "q":
                        B.dma("pool", ql_d[row0:row0 + 128, :], fst[fb][:, :], reads=[fstB[fb]], writes=[qlB],
                              part=True)
                    else:
                        pn = dst[1:]
                        B.dma("pool", kview(kvl_d[l][pn], pn)[row0:row0 + 128, :], fst[fb][:, :],
                              reads=[fstB[fb]], writes=[kvlB[l][pn]], part=True)
                elif kd[0] in ("V", "V1"):
                    _, pn, vcol0, nh, hw, width = kd
                    c0 = 128 if kd[0] == "V1" else 0
                    if hw == 72:
                        B.op("pool", lambda e, nh=nh: e.memset(vst[:, :, 0:nh * 72], 1.0), writes=[vstB])
                    for j in range(NSLOT):
                        pb = pcount % 4
                        pcount += 1
                        B._deps("pe", [wbfB[b], hTB], [psB[pb]])
                        for c in range(8):
                            fn = lambda e, c=c, pb=pb, b=b, c0=c0, j=j, width=width: e.matmul(
                                ps[pb][:, 0:width], lhsT=hT[:, c, j * 128:(j + 1) * 128],
                                rhs=wbf[b][:, c, c0:c0 + width], start=(c == 0), stop=(c == 7))
                            if c < 7:
                                B.pe_quiet(fn)
                            else:
                                B.op("pe", fn, reads=[wbfB[b], hTB], writes=[psB[pb]])
                        B.op("dve", lambda e, pb=pb, j=j, nh=nh, hw=hw, width=width: e.tensor_copy(
                            out=vst[:, j, 0:nh * hw].rearrange("p (h w) -> p h w", h=nh)[:, :, 0:64],
                            in_=ps[pb][:, 0:width].rearrange("p (h w) -> p h w", h=nh)),
                            reads=[psB[pb]], writes=[vstB])
                    B.dma("pool", vview(kvl_d[l][pn], pn)[:, vcol0:vcol0 + nh * hw].rearrange("(j p) w -> p j w", p=128),
                          vst[:, :, 0:nh * hw], reads=[vstB], writes=[kvlB[l][pn]], part=True)
                else:
                    _, gcol0, width = kd
                    for j in range(NSLOT):
                        pb = pcount % 4
                        pcount += 1
                        B._deps("pe", [wbfB[b], hTB], [psB[pb]])
                        for c in range(8):
                            fn = lambda e, c=c, pb=pb, b=b, j=j, width=width: e.matmul(
                                ps[pb][:, 0:width], lhsT=hT[:, c, j * 128:(j + 1) * 128],
                                rhs=wbf[b][:, c, 0:width], start=(c == 0), stop=(c == 7))
                            if c < 7:
                                B.pe_quiet(fn)
                            else:
                                B.op("pe", fn, reads=[wbfB[b], hTB], writes=[psB[pb]])
                        B.op("act", lambda e, pb=pb, j=j, gcol0=gcol0, width=width: e.activation(
                            out=gy[:, j, gcol0:gcol0 + width], in_=ps[pb][:, 0:width], func=AF.Silu),
                            reads=[psB[pb]], writes=[gyB[j]])
            if gather_after is not None:
                gather(l, gather_after)
        copy_mine(l, "A")
        B.barrier()
        B.stack.close()
        B.stack = old_stack

    def gather(l, pn):
        B._deps("pool", [kvlB[l][pn]], [kvfB[l][pn]])
        ins = nc.gpsimd.collective_compute("AllGather", ALU.bypass,
                                           replica_groups=[[0, 1, 2, 3, 4, 5, 6, 7]],
                                           ins=[kvl_d[l][pn][:, :]], outs=[kvf_d[l][pn][:, :]])
        B.cccnt += 1
        ins.then_inc(B.ccsem, 1)
        tok = ("cc", B.ccsem, B.cccnt, "cc")
        B._mark(tok, [kvlB[l][pn]], [kvfB[l][pn]])

    def copy_mine(l, pn, q="sp"):
        kr, vw = PARTS[pn]
        rows = 4 * (kr + vw)
        bidx = B.eng[q].partition_id() // 4
        src_ = kvf_d[l][pn].rearrange("(b x) n -> b x n", b=2)[bass.ds(bidx, 1), :, :][0]
        nsp = 4
        step = rows // nsp
        for p_ in range(nsp):
            B.dma(q, kvm_d[l][pn][p_ * step:(p_ + 1) * step, :], src_[p_ * step:(p_ + 1) * step, :],
                  reads=[kvfB[l][pn]], writes=[kvmB[l][pn]], part=True)

    def slot_info(g):
        res = []
        for t in range(4):
            j = 4 * g + t
            m, s = divmod(j, 2)
            L = 8 * m + 4 + 4 * s
            res.append((t, j, L, s))
        return res

    def phase_attn(l):
        B.barrier()
        old_stack = B.stack
        B.stack = contextlib.ExitStack()
        lambda_init = 0.8 - 0.6 * math.exp(-0.3 * l)
        maskA = B.sb([128, 1024], BF16)
        maskC = B.sb([128, 1024], BF16)
        m01C = B.sb([128, 1024], BF16)
        maskB = B.sb([128, 1280], BF16)
        negU = B.sb([128, 128], BF16)
        negSel = B.sb([64, 128], BF16)
        ones64 = B.sb([128, 64], BF16)
        cB = Buf()
        for dst, src in ((maskA, maskA_d), (maskC, maskC_d), (m01C, m01C_d), (maskB, maskB_d), (negU, negU_d),
                         (negSel, negSel_d), (ones64, ones64_d)):
            B.dma("sp", dst[:, :], src[:, :], writes=[cB], part=True)
        lamt = B.sb([128, 128], F32)
        subg = B.sb([128, 64], F32)
        sinkt = B.sb([128, 8], F32)
        B.dma("sp", lamt[:, :], lam_d[l, :, :], writes=[cB], part=True)
        B.dma("sp", subg[:, :], subg_d[l, :, :], writes=[cB], part=True)
        B.dma("sp", sinkt[:, :], sink_d[l, :, :], writes=[cB], part=True)
        sm = B.sb([128, 16], F32)
        smB = Buf()
        prod = B.sb([128, 64], F32)
        B.op("dve", lambda e: e.tensor_tensor(out=prod[:, 0:32], in0=lamt[:, 0:32], in1=lamt[:, 32:64], op=ALU.mult),
             reads=[cB], writes=[smB])
        B.op("dve", lambda e: e.tensor_tensor(out=prod[:, 32:64], in0=lamt[:, 64:96], in1=lamt[:, 96:128],
                                              op=ALU.mult), reads=[cB, smB], writes=[smB])
        B.op("dve", lambda e: e.reduce_sum(out=sm[:, 0:1], in_=prod[:, 0:32], axis=AX.X), reads=[smB], writes=[smB])
        B.op("dve", lambda e: e.reduce_sum(out=sm[:, 1:2], in_=prod[:, 32:64], axis=AX.X), reads=[smB], writes=[smB])
        B.op("act", lambda e: e.activation(out=sm[:, 2:4], in_=sm[:, 0:2], func=AF.Exp), reads=[smB], writes=[smB])
        B.op("dve", lambda e: e.tensor_tensor(out=sm[:, 4:5], in0=sm[:, 3:4], in1=sm[:, 2:3], op=ALU.subtract),
             reads=[smB], writes=[smB])
        B.op("dve", lambda e: e.tensor_scalar(out=sm[:, 4:5], in0=sm[:, 4:5], scalar1=-float(lambda_init),
                                              scalar2=None, op0=ALU.add), reads=[smB], writes=[smB])
        B.op("dve", lambda e: e.tensor_scalar(out=subg[:, :], in0=subg[:, :], scalar1=float(1.0 - lambda_init),
                                              scalar2=None, op0=ALU.mult), reads=[cB, smB], writes=[smB])
        B.op("act", lambda e: e.activation(out=sm[:, 8:16], in_=sinkt[:, :], func=AF.Exp), reads=[cB, smB],
             writes=[smB])
        constB = [cB, smB, identB]

        KT = [B.sb([128, 8192], BF16) for _ in range(2)]
        KTB = [Buf(), Buf()]
        QT = [B.sb([128, TOK], BF16) for _ in range(2)]
        QT1 = [B.sb([128, TOK], BF16) for _ in range(2)]
        QTB = [Buf(), Buf()]
        for b_ in range(2):
            B.op("pool", lambda e, b_=b_: e.memset(KT[b_][:, :], 0.0), writes=[KTB[b_]])
            B.op("pool", lambda e, b_=b_: e.memset(QT[b_][:, :], 0.0), writes=[QTB[b_]])
            B.op("pool", lambda e, b_=b_: e.memset(QT1[b_][:, :], 0.0), writes=[QTB[b_]])
        Vt = [B.sb([128, 64, 128], BF16) for _ in range(2)]
        VtB = [Buf(), Buf()]
        PT = [B.sb([128, 512], BF16) for _ in range(4)]
        PTB = [Buf() for _ in range(4)]
        Et = [B.sb([128, 512], F32) for _ in range(2)]
        EtB = [Buf(), Buf()]
        SPt = [B.sb([128, 512], BF16) for _ in range(2)]
        SPB = [Buf(), Buf()]
        Rt = B.sb([64, 512], BF16)
        RB = Buf()
        ep = B.sb([128, 8, 64], F32)
        epB = Buf()
        es = B.sb([128, 16], F32)
        esB = Buf()
        hcount = [0]

        def load_head(kind, h, bparts=("kv", "q")):
            if kind == "B":
                b = 0
            else:
                b = hcount[0] % 2
                hcount[0] += 1
            kt, qt, vt = KT[b], QT[b], Vt[b]
            mA, mBC = kvm_d[l]["A"], kvm_d[l]["BC"]
            mAB, mBCB = kvmB[l]["A"], kvmB[l]["BC"]

            def vload(pn, dst_cols, c0, c1, wB):
                mt = mA if pn == "A" else mBC
                for rr in range(4):
                    v3 = vview(mt, pn, rr).rearrange("(j p) w -> p j w", p=128)
                    B.dma("sp", vt[:, rr * 16:(rr + 1) * 16, dst_cols[0]:dst_cols[1]], v3[:, :, c0:c1],
                          reads=[mAB if pn == "A" else mBCB], writes=[wB], part=True)

            if kind == "A":
                for m in range(2):
                    r0 = (h * 2 + m) * 32
                    for rr in range(4):
                        B.dma("sp", kt[m * 64:m * 64 + 32, rr * TOK:(rr + 1) * TOK],
                              kview(mA, "A", rr)[r0:r0 + 32, :], reads=[mAB], writes=[KTB[b]], part=True)
                    B.dma("sp", kt[m * 64 + 32:m * 64 + 38, :], kaugA_d[h, :, :], writes=[KTB[b]], part=True)
                    qtm = qt if m == 0 else QT1[b]
                    B.dma("sp", qtm[m * 64:m * 64 + 32, :], ql_d[r0:r0 + 32, :], reads=[qlB], writes=[QTB[b]], part=True)
                    B.dma("sp", qtm[m * 64 + 32:m * 64 + 38, :], qaugA_d[h, :, :], writes=[QTB[b]], part=True)
                vload("A", (0, 72), h * 72, (h + 1) * 72, VtB[b])
            elif kind == "C":
                for m in range(2):
                    hh = h * 2 + m
                    for rr in range(4):
                        B.dma("sp", kt[m * 64:m * 64 + 64, rr * TOK:(rr + 1) * TOK],
                              kview(mBC, "BC", rr)[128 + hh * 64:128 + (hh + 1) * 64, :],
                              reads=[mBCB], writes=[KTB[b]], part=True)
                    qtm = qt if m == 0 else QT1[b]
                    B.dma("sp", qtm[m * 64:m * 64 + 64, :], ql_d[768 + hh * 64:768 + (hh + 1) * 64, :], reads=[qlB],
                          writes=[QTB[b]], part=True)
                for m in range(2):
                    vload("BC", (m * 64, (m + 1) * 64), 144 + h * 128 + m * 64, 144 + h * 128 + (m + 1) * 64, VtB[b])
            else:
                qx = KT[1]
                if "kv" in bparts:
                    ztok = B.op("dve", lambda e: e.memset(kt[64:128, :], 0.0), writes=[KTB[0]])
                    B._wait("sp", ztok)
                    for rr in range(4):
                        B.dma("sp", kt[0:64, rr * TOK:(rr + 1) * TOK],
                              kview(mBC, "BC", rr)[h * 64:(h + 1) * 64, :], reads=[mBCB], writes=[KTB[0]], part=True)
                    B.dma("sp", kt[64:73, :], kaugB_d[:, :], writes=[KTB[0]], part=True)
                    vload("BC", (0, 72), h * 72, (h + 1) * 72, VtB[0])
                if "q" in bparts:
                    for hh in range(4):
                        r0 = 256 + (h * 4 + hh) * 64
                        B.dma("sp", qx[0:64, :].rearrange("p (j hh i) -> p j hh i", j=16, hh=4)[:, :, hh, :],
                              ql_d[r0:r0 + 64, :].rearrange("p (j i) -> p j i", j=16), reads=[qlB], writes=[KTB[1]],
                              part=True)
                    B.dma("sp", qx[64:73, :], qaugB_d[h, :, :], writes=[KTB[1]], part=True)
            return b

        mhalf = B.sb([128, 4], F32)
        B.op("pool", lambda e: e.memset(mhalf[:, :], -0.5), writes=[cB])
        Rt2 = [B.sb([128, 512], BF16), B.sb([128, 512], BF16)]
        RB2 = [Buf(), Buf()]
        ones128 = B.sb([128, 128], BF16)
        negSel128 = B.sb([128, 128], BF16)
        B.op("pool", lambda e: e.memset(ones128[:, :], 1.0), writes=[cB])
        B.op("pool", lambda e: e.memset(negSel128[:, :], 0.0), writes=[cB])
        B.op("pool", lambda e: e.memset(negSel128[0:1, :], -1.0), writes=[cB])
        hb = {}

        def emit_A():
            units = []
            for h in range(4):
                for g in range(4):
                    slots = slot_info(g)
                    nk = slots[3][2]
                    for kb in range(nk):
                        for m in range(2):
                            units.append((h, g, kb, m, slots, nk))
            first_pv = {}

            def ensure(h):
                if h < 4 and ("A", h) not in hb:
                    hb[("A", h)] = load_head("A", h)

            def qk_exp(u, i):
                h, g, kb, m, slots, nk = u
                b = hb[("A", h)]
                kt, qt = KT[b], QT[b]
                act_slots = [s_ for s_ in slots if kb < s_[2]]
                t0 = act_slots[0][0]
                pos = pos_of(kb)
                pb = i % 4
                near = [s_ for s_ in act_slots if kb >= s_[2] - 4]
                rds = [KTB[b], QTB[b]] + constB
                B._deps("pe", rds, [psB[pb]])
                fn = lambda e: e.matmul(
                    ps[pb][:, t0 * 128:512], lhsT=kt[:, pos * 128:(pos + 1) * 128],
                    rhs=(qt if m == 0 else QT1[b])[:, (4 * g + t0) * 128:(4 * g + 4) * 128], start=True,
                    stop=(len(near) == 0))
                if near:
                    B.pe_quiet(fn)
                    for ni, (t, j, L, s) in enumerate(near):
                        mi = s * 4 + (kb - (L - 4))
                        fn2 = lambda e, t=t, mi=mi, last=(ni == len(near) - 1): e.matmul(
                            ps[pb][:, t * 128:(t + 1) * 128], lhsT=ident[:, :],
                            rhs=maskA[:, mi * 128:(mi + 1) * 128], start=False, stop=last)
                        if ni < len(near) - 1:
                            B.pe_quiet(fn2)
                        else:
                            B.op("pe", fn2, reads=rds, writes=[psB[pb]])
                else:
                    B.op("pe", fn, reads=rds, writes=[psB[pb]])
                B.op("act", lambda e: e.activation(out=PT[pb][:, t0 * 128:512], in_=ps[pb][:, t0 * 128:512],
                                                   func=AF.Exp), reads=[psB[pb]], writes=[PTB[pb]])

            def pv(u, i):
                h, g, kb, m, slots, nk = u
                b = hb[("A", h)]
                vt = Vt[b]
                act_slots = [s_ for s_ in slots if kb < s_[2]]
                pos = pos_of(kb)
                pti = i % 4
                par = (h * 4 + g) % 2
                ob = 4 + 2 * par + m
                B._deps("pe", [PTB[pti], VtB[b]], [psB[ob]])
                for ai, (t, j, L, s) in enumerate(act_slots):
                    st = first_pv.get((h, g, m), True)
                    first_pv[(h, g, m)] = False
                    fn = lambda e, t=t, st=st, sp_=(kb == L - 1): e.matmul(
                        ps[ob][:, t * 72:t * 72 + 65], lhsT=PT[pti][:, t * 128:(t + 1) * 128],
                        rhs=vt[:, pos, 0:65], start=st, stop=sp_, skip_group_check=True)
                    if ai < len(act_slots) - 1:
                        B.pe_quiet(fn)
                    else:
                        B.op("pe", fn, reads=[PTB[pti], VtB[b]], writes=[psB[ob]])

            def epilogue(u):
                h, g, kb, m, slots, nk = u
                par = (h * 4 + g) % 2
                Ob = [4 + 2 * par, 5 + 2 * par]
                O0, O1 = ps[Ob[0]], ps[Ob[1]]
                rd = [psB[Ob[0]], psB[Ob[1]]] + constB
                o0v = O0[:, 0:288].rearrange("p (t w) -> p t w", t=4)
                o1v = O1[:, 0:288].rearrange("p (t w) -> p t w", t=4)
                B.op("dve", lambda e: e.reciprocal(out=es[:, 0:4], in_=o0v[:, :, 64]), reads=rd, writes=[esB])
                B.op("dve", lambda e: e.reciprocal(out=es[:, 4:8], in_=o1v[:, :, 64]), reads=rd + [esB], writes=[esB])
                for t in range(4):
                    B.op("dve", lambda e, t=t: e.tensor_scalar(out=ep[:, t, :], in0=o0v[:, t, 0:64],
                                                               scalar1=es[:, t:t + 1], scalar2=None, op0=ALU.mult),
                         reads=rd + [esB], writes=[epB])
                    B.op("dve", lambda e, t=t: e.tensor_scalar(out=ep[:, 4 + t, :], in0=o1v[:, t, 0:64],
                                                               scalar1=es[:, 4 + t:5 + t], scalar2=sm[:, 4:5],
                                                               op0=ALU.mult, op1=ALU.mult),
                         reads=rd + [esB, epB], writes=[epB])
                    B.op("dve", lambda e, t=t: e.tensor_tensor(out=ep[:, t, :], in0=ep[:, t, :], in1=ep[:, 4 + t, :],
                                                               op=ALU.add), reads=[epB], writes=[epB])
                    B.op("dve", lambda e, t=t: e.scalar_tensor_tensor(out=ep[:, 4 + t, :], in0=ep[:, t, :],
                                                                      scalar=1.0 / 64, in1=ep[:, t, :],
                                                                      op0=ALU.mult, op1=ALU.mult,
                                                                      accum_out=es[:, 8 + t:9 + t]),
                         reads=[epB, esB], writes=[epB, esB])
                B.op("dve", lambda e: e.tensor_scalar(out=es[:, 8:12], in0=es[:, 8:12], scalar1=EPS, scalar2=None,
                                                      op0=ALU.add), reads=[esB], writes=[esB])
                B.op("pool", lambda e: e.tensor_tensor(out=es[:, 12:16], in0=es[:, 8:12], in1=mhalf[:, :],
                                                       op=ALU.pow), reads=[esB, cB], writes=[esB])
                for t in range(4):
                    j = 4 * g + t
                    B.op("dve", lambda e, t=t: e.scalar_tensor_tensor(out=ep[:, t, :], in0=ep[:, t, :],
                                                                      scalar=es[:, 12 + t:13 + t], in1=subg[:, :],
                                                                      op0=ALU.mult, op1=ALU.mult),
                         reads=[epB, esB] + constB, writes=[epB])
                    B.op("dve", lambda e, t=t, j=j: e.tensor_tensor(out=gy[:, j, h * 64:(h + 1) * 64],
                                                                    in0=ep[:, t, :], in1=gy[:, j, h * 64:(h + 1) * 64],
                                                                    op=ALU.mult),
                         reads=[epB, gyB[j]], writes=[gyB[j]])

            ensure(0)
            DD = 2
            n = len(units)
            for i in range(n + DD):
                if i < n:
                    u = units[i]
                    if u[1] == 0 and u[2] == 0 and u[3] == 0:
                        ensure(u[0])
                    if u[1] == 1 and u[2] == 0 and u[3] == 0:
                        ensure(u[0] + 1)
                        if u[0] == 2:
                            copy_mine(l, "BC")
                    if u[0] == 3 and u[1] == 1 and u[2] == 0 and u[3] == 0:
                        load_head("B", 0, bparts=("kv",))
                    qk_exp(u, i)
                if i >= DD:
                    u = units[i - DD]
                    pv(u, i - DD)
                    if u[2] == u[5] - 1 and u[3] == 1:
                        epilogue(u)

        def emit_B(g2, kv_preloaded=False):
            load_head("B", g2, bparts=("q",) if kv_preloaded else ("kv", "q"))
            kt, vt = KT[0], Vt[0]
            qx = KT[1]
            units = []
            for j in range(NSLOT):
                m_, s = divmod(j, 2)
                base = 8 * m_ + 4 * s
                steps = [rel for rel in range(5) if base + rel - 1 >= 0]
                for si, rel in enumerate(steps):
                    units.append((j, s, base + rel - 1, rel, si == 0, si == len(steps) - 1))

            def qk_exp(u, i):
                j, s, kb, rel, first, last = u
                pos = pos_of(kb)
                pb = i % 4
                mi = s * 5 + rel
                rds = [KTB[0], KTB[1]] + constB
                B._deps("pe", rds, [psB[pb]])
                B.pe_quiet(lambda e: e.matmul(
                    ps[pb][:, :], lhsT=kt[:, pos * 128:(pos + 1) * 128], rhs=qx[:, j * 512:(j + 1) * 512],
                    start=True, stop=False))
                for hh in range(4):
                    fn2 = lambda e, hh=hh: e.matmul(
                        ps[pb][:, hh * 128:(hh + 1) * 128], lhsT=ident[:, :],
                        rhs=maskB[:, mi * 128:(mi + 1) * 128], start=False, stop=(hh == 3))
                    if hh < 3:
                        B.pe_quiet(fn2)
                    else:
                        B.op("pe", fn2, reads=rds, writes=[psB[pb]])
                B.op("act", lambda e: e.activation(out=PT[pb][:, :], in_=ps[pb][:, :], func=AF.Exp),
                     reads=[psB[pb]], writes=[PTB[pb]])

            def pv(u, i):
                j, s, kb, rel, first, last = u
                pos = pos_of(kb)
                pti = i % 4
                ob = 4 + (j % 2)
                B._deps("pe", [PTB[pti], VtB[0]], [psB[ob]])
                for hh in range(4):
                    fn = lambda e, hh=hh, st=(first and hh == 0): e.matmul(
                        ps[ob][:, hh * 72:hh * 72 + 65], lhsT=PT[pti][:, hh * 128:(hh + 1) * 128],
                        rhs=vt[:, pos, 0:65], start=st, stop=last, skip_group_check=True)
                    if hh < 3:
                        B.pe_quiet(fn)
                    else:
                        B.op("pe", fn, reads=[PTB[pti], VtB[0]], writes=[psB[ob]])
                if last:
                    ov = ps[ob][:, 0:288].rearrange("p (t w) -> p t w", t=4)
                    rd = [psB[ob]] + constB
                    B.op("dve", lambda e: e.tensor_tensor(out=es[:, 0:4], in0=ov[:, :, 64],
                                                          in1=sm[:, 8 + g2 * 4:12 + g2 * 4], op=ALU.add),
                         reads=rd + [esB], writes=[esB])
                    B.op("dve", lambda e: e.reciprocal(out=es[:, 0:4], in_=es[:, 0:4]), reads=[esB], writes=[esB])
                    for hh in range(4):
                        c0 = 256 + (g2 * 4 + hh) * 64
                        B.op("dve", lambda e, hh=hh, c0=c0: e.scalar_tensor_tensor(
                            out=gy[:, j, c0:c0 + 64], in0=ov[:, hh, 0:64], scalar=es[:, hh:hh + 1],
                            in1=gy[:, j, c0:c0 + 64], op0=ALU.mult, op1=ALU.mult),
                            reads=rd + [esB, gyB[j]], writes=[gyB[j]])

            DD = 2
            n = len(units)
            for i in range(n + DD):
                if i < n:
                    qk_exp(units[i], i)
                if i >= DD:
                    pv(units[i - DD], i - DD)

        def emit_C():
            units = []
            gi = 0
            for hp in range(2):
                for m in range(2):
                    for g in range(4):
                        slots = slot_info(g)
                        nk = slots[3][2]
                        for kb in range(nk - 1, -1, -1):
                            units.append((hp, m, g, kb, slots, nk, gi))
                        gi += 1

            def ensure(hp):
                if hp < 2 and ("C", hp) not in hb:
                    hb[("C", hp)] = load_head("C", hp)

            def geom(u):
                hp, m, g, kb, slots, nk, gi = u
                act_slots = [s_ for s_ in slots if kb < s_[2]]
                t0 = act_slots[0][0]
                near = [s_ for s_ in act_slots if kb >= s_[2] - 4]
                newslots = [s_ for s_ in act_slots if kb == s_[2] - 1]
                return act_slots, t0 * 128, near, newslots

            def stA(u, i):
                hp, m, g, kb, slots, nk, gi = u
                b = hb[("C", hp)]
                kt, qt = KT[b], QT[b]
                p0 = m * 64
                act_slots, c0, near, newslots = geom(u)
                pos = pos_of(kb)
                zb = i % 2
                kq = [KTB[b], QTB[b]] + constB
                B.op("pe", lambda e: e.matmul(
                    ps[zb][:, c0:512], lhsT=kt[:, pos * 128:(pos + 1) * 128],
                    rhs=(qt if m == 0 else QT1[b])[:, 4 * g * 128 + c0:(4 * g + 4) * 128], start=True, stop=True),
                    reads=kq, writes=[psB[zb]])
                B.op("act", lambda e: e.activation(out=ps[zb][:, c0:512], in_=ps[zb][:, c0:512], func=AF.Exp),
                     reads=[psB[zb]], writes=[psB[zb]])
                B.op("act", lambda e: e.activation(out=SPt[zb][:, c0:512], in_=ps[zb][:, c0:512], func=AF.Ln,
                                                   bias=1.0), reads=[psB[zb]], writes=[SPB[zb]])
                for (t, j, L, s) in near:
                    mi = s * 4 + (kb - (L - 4))
                    B.op("dve", lambda e, t=t, mi=mi: e.tensor_tensor(
                        out=SPt[zb][:, t * 128:(t + 1) * 128], in0=SPt[zb][:, t * 128:(t + 1) * 128],
                        in1=m01C[:, mi * 128:(mi + 1) * 128], op=ALU.mult),
                        reads=[SPB[zb]] + constB, writes=[SPB[zb]])

            def stB(u, i):
                hp, m, g, kb, slots, nk, gi = u
                b = hb[("C", hp)]
                kt, qt = KT[b], QT[b]
                p0 = m * 64
                act_slots, c0, near, newslots = geom(u)
                pos = pos_of(kb)
                zb = i % 2
                ab = 2 + zb
                first = (kb == nk - 1)
                kq = [KTB[b], QTB[b]] + constB
                rprev, rprevB = Rt2[(i + 1) % 2], RB2[(i + 1) % 2]
                rcur, rcurB = Rt2[i % 2], RB2[i % 2]
                B.op("pe", lambda e: e.matmul(
                    ps[4][:, c0:512], lhsT=ones128[:, :], rhs=SPt[zb][:, c0:512], start=first, stop=(kb == 0),
                    skip_group_check=True), reads=[SPB[zb]] + constB, writes=[psB[4]])
                if kb > 0:
                    B.op("dve", lambda e: e.tensor_copy(out=rcur[:, c0:512], in_=ps[4][:, c0:512]),
                         reads=[psB[4]], writes=[rcurB])
                for (t, j, L, s) in newslots:
                    B.op("pool", lambda e, t=t: e.memset(rprev[:, t * 128:(t + 1) * 128], 0.0), writes=[rprevB])
                B._deps("pe", kq + [SPB[zb], rprevB], [psB[ab]])
                B.pe_quiet(lambda e: e.matmul(
                    ps[ab][:, c0:512], lhsT=kt[:, pos * 128:(pos + 1) * 128],
                    rhs=(qt if m == 0 else QT1[b])[:, 4 * g * 128 + c0:(4 * g + 4) * 128], start=True, stop=False))
                B.pe_quiet(lambda e: e.matmul(
                    ps[ab][:, c0:512], lhsT=negU[:, :], rhs=SPt[zb][:, c0:512], start=False, stop=False))
                for (t, j, L, s) in near:
                    mi = s * 4 + (kb - (L - 4))
                    B.pe_quiet(lambda e, t=t, mi=mi: e.matmul(
                        ps[ab][:, t * 128:(t + 1) * 128], lhsT=ident[:, :],
                        rhs=maskC[:, mi * 128:(mi + 1) * 128], start=False, stop=False))
                B.op("pe", lambda e: e.matmul(
                    ps[ab][:, c0:512], lhsT=negSel128[:, :], rhs=rprev[:, c0:512], start=False, stop=True),
                    reads=kq + [SPB[zb], rprevB], writes=[psB[ab]])
                B.op("act", lambda e: e.activation(out=PT[zb][:, c0:512], in_=ps[ab][:, c0:512], func=AF.Exp),
                     reads=[psB[ab]], writes=[PTB[zb]])

            def stC(u, i):
                hp, m, g, kb, slots, nk, gi = u
                b = hb[("C", hp)]
                vt = Vt[b]
                v0 = m * 64
                hh = hp * 2 + m
                act_slots, c0, near, newslots = geom(u)
                pos = pos_of(kb)
                pti = i % 2
                ob = 5 + gi % 2
                first = (kb == nk - 1)
                B._deps("pe", [PTB[pti], VtB[b]], [psB[ob]])
                for ai, (t, j, L, s) in enumerate(act_slots):
                    fn = lambda e, t=t, st=(first and ai == 0): e.matmul(
                        ps[ob][:, t * 64:(t + 1) * 64], lhsT=PT[pti][:, t * 128:(t + 1) * 128],
                        rhs=vt[:, pos, v0:v0 + 64], start=st, stop=(kb == 0), skip_group_check=True)
                    if ai < len(act_slots) - 1:
                        B.pe_quiet(fn)
                    else:
                        B.op("pe", fn, reads=[PTB[pti], VtB[b]], writes=[psB[ob]])
                if kb == 0:
                    for t in range(4):
                        j = 4 * g + t
                        cc = 768 + hh * 64
                        B.op("dve", lambda e, t=t, j=j, cc=cc: e.tensor_tensor(
                            out=gy[:, j, cc:cc + 64], in0=ps[ob][:, t * 64:(t + 1) * 64], in1=gy[:, j, cc:cc + 64],
                            op=ALU.mult), reads=[psB[ob], gyB[j]], writes=[gyB[j]])

            ensure(0)
            n = len(units)
            for i in range(n + 2):
                if i < n:
                    u = units[i]
                    if u[1] == 0 and u[2] == 0 and u[3] == u[5] - 1:
                        ensure(u[0])
                    if u[1] == 0 and u[2] == 1 and u[3] == u[5] - 1:
                        ensure(u[0] + 1)
                    stA(u, i)
                if 1 <= i <= n:
                    stB(units[i - 1], i - 1)
                if i >= 2:
                    stC(units[i - 2], i - 2)

        emit_A()
        emit_B(0, kv_preloaded=True)
        emit_B(1)
        hcount[0] = 0
        emit_C()
        B.barrier()
        B.stack.close()
        B.stack = old_stack

    def phase_out(l):
        B.barrier()
        old_stack = B.stack
        B.stack = contextlib.ExitStack()
        wo = B.sb([128, 8, D], BF16)
        woB = Buf()
        wst = [B.sb([128, 8, 256], F32) for _ in range(2)]
        wstB = [Buf(), Buf()]
        wv = wout_d[l, :, :].rearrange("(c p) n -> p c n", p=128)
        for ci in range(4):
            b = ci % 2
            B.dma("sp", wst[b][:, :, :], wv[:, :, ci * 256:(ci + 1) * 256], writes=[wstB[b]])
            B.op("act", lambda e, b=b, ci=ci: e.copy(out=wo[:, :, ci * 256:(ci + 1) * 256], in_=wst[b][:, :, :]),
                 reads=[wstB[b]], writes=[woB])
        yT = [B.sb([128, 8, 128], BF16) for _ in range(2)]
        yTB = [Buf(), Buf()]
        for j in range(NSLOT):
            b = j % 2
            pb = 6 + b
            tp = ps[pb][:, :].bitcast(BF16)
            B._deps("pe", [gyB[j], identB], [psB[pb]])
            for c in range(8):
                fn = lambda e, c=c, j=j, tp=tp: e.transpose(out=tp[:, c * 128:(c + 1) * 128],
                                                           in_=gy[:, j, c * 128:(c + 1) * 128], identity=ident[:, :])
                if c < 7:
                    B.pe_quiet(fn)
                else:
                    B.op("pe", fn, reads=[gyB[j], identB], writes=[psB[pb]])
            B.op("act", lambda e, b=b, tp=tp: e.copy(out=yT[b][:, :, :], in_=tp.rearrange("p (c t) -> p c t", c=8)),
                 reads=[psB[pb]], writes=[yTB[b]])
            for half in range(2):
                ob = 2 * b + half
                B._deps("pe", [yTB[b], woB], [psB[ob]])
                for c in range(8):
                    fn = lambda e, c=c, b=b, ob=ob, half=half: e.matmul(
                        ps[ob][:, :], lhsT=yT[b][:, c, :], rhs=wo[:, c, half * 512:(half + 1) * 512],
                        start=(c == 0), stop=(c == 7))
                    if c < 7:
                        B.pe_quiet(fn)
                    else:
                        B.op("pe", fn, reads=[yTB[b], woB], writes=[psB[ob]])
                B.op("dve", lambda e, j=j, ob=ob, half=half: e.tensor_tensor(
                    out=x_sb[:, j, half * 512:(half + 1) * 512], in0=ps[ob][:, :],
                    in1=x_sb[:, j, half * 512:(half + 1) * 512], op=ALU.add),
                    reads=[psB[ob], xB[j]], writes=[xB[j]])
        B.barrier()
        B.stack.close()
        B.stack = old_stack

    def phase_final():
        B.barrier()
        gbc = B.sb([128, D], F32)
        gB = Buf()
        B.dma("sp", gbc[:, :], fg_d[:, :], writes=[gB])
        ssq = B.sb([128, NSLOT], F32)
        rstd = B.sb([128, NSLOT], F32)
        junk = B.sb([128, D], F32)
        junkB, ssqB, rstdB = Buf(), Buf(), Buf()
        ot = [B.sb([128, D], F32) for _ in range(2)]
        otB = [Buf(), Buf()]
        for j in range(NSLOT):
            B.op("act", lambda e, j=j: e.activation(out=junk[:, :], in_=x_sb[:, j, :], func=AF.Square,
                                                    accum_out=ssq[:, j:j + 1]),
                 reads=[xB[j]], writes=[junkB, ssqB])
        B.op("act", lambda e: e.activation(out=rstd[:, :], in_=ssq[:, :], func=AF.Ln, scale=1.0 / D, bias=EPS),
             reads=[ssqB], writes=[rstdB])
        B.op("act", lambda e: e.activation(out=rstd[:, :], in_=rstd[:, :], func=AF.Exp, scale=-0.5),
             reads=[rstdB], writes=[rstdB])
        outB = Buf()
        for j in range(NSLOT):
            b = j % 2
            B.op("dve", lambda e, j=j, b=b: e.scalar_tensor_tensor(out=ot[b][:, :], in0=x_sb[:, j, :],
                                                                   scalar=rstd[:, j:j + 1], in1=gbc[:, :],
                                                                   op0=ALU.mult, op1=ALU.mult),
                 reads=[xB[j], rstdB, gB], writes=[otB[b]])
            B.dma("sp", out_d[j, :, :], ot[b][:, :], reads=[otB[b]], writes=[outB])
        B.barrier()

    for l in range(DEPTH):
        if l < nlay_proj:
            phase_proj(l)
        if l < nlay_attn:
            phase_attn(l)
            phase_out(l)
            if stage == 2 and l == 0:
                dbgB = Buf()
                for j in range(NSLOT):
                    B.dma("sp", xdbg_d[j, :, :], x_sb[:, j, :], reads=[xB[j]], writes=[dbgB], part=True)
    if stage >= 3:
        phase_final()
    else:
        B.barrier()
    return nc


_NC_CACHE = {}


def _get_nc(stage):
    if stage not in _NC_CACHE:
        _NC_CACHE[stage] = build(stage)
    return _NC_CACHE[stage]


def _core_inputs(c, inputs):
    b, r = divmod(c, 4)
    x = np.asarray(inputs["x"], np.float32)
    xs = np.stack([x[b, qb_of(r, j) * 128:(qb_of(r, j) + 1) * 128, :] for j in range(NSLOT)])
    d = {"x": np.ascontiguousarray(xs)}
    d["w_in"] = np.ascontiguousarray(np.asarray(inputs["w_in"], np.float32))
    d["w_out"] = np.ascontiguousarray(np.asarray(inputs["w_out"], np.float32))
    d["ng"] = np.ascontiguousarray(np.broadcast_to(np.asarray(inputs["norm_g"], np.float32)[:, None, :], (DEPTH, 128, D)))
    d["fg"] = np.ascontiguousarray(np.broadcast_to(np.asarray(inputs["final_g"], np.float32)[None, :], (128, D)))
    lam4 = np.concatenate([np.asarray(inputs[k], np.float32) for k in ("lambda_q1", "lambda_k1", "lambda_q2", "lambda_k2")], axis=1)
    d["lam4"] = np.ascontiguousarray(np.broadcast_to(lam4[:, None, :], (DEPTH, 128, 128)))
    d["subg"] = np.ascontiguousarray(np.broadcast_to(np.asarray(inputs["subln_g"], np.float32)[:, None, :], (DEPTH, 128, 64)))
    d["sinks"] = np.ascontiguousarray(np.broadcast_to(np.asarray(inputs["sinks"], np.float32)[:, None, :], (DEPTH, 128, 8)))
    d.update(host_tables(r))
    return d


def _gather_host(res, name):
    out = []
    for c in range(8):
        grp = (c // 4) * 4
        out.append(np.concatenate([np.asarray(res[grp + rr][name]) for rr in range(4)], axis=0))
    return out


def kernel(**inputs):
    ins = [_core_inputs(c, inputs) for c in range(8)]
    cores = list(range(8))
    FUSED = True
    if FUSED:
        res = run_bass_kernel_spmd(_get_nc(4), ins, core_ids=cores).results
    else:
        r1 = run_bass_kernel_spmd(_get_nc(1), ins, core_ids=cores).results
        k0 = _gather_host(r1, "ktl0")
        v0 = _gather_host(r1, "vl0")
        for c in range(8):
            ins[c]["kta0"] = k0[c]
            ins[c]["va0"] = v0[c]
        r2 = run_bass_kernel_spmd(_get_nc(2), ins, core_ids=cores).results
        k1 = _gather_host(r2, "ktl1")
        v1 = _gather_host(r2, "vl1")
        for c in range(8):
            ins[c]["kta1"] = k1[c]
            ins[c]["va1"] = v1[c]
        res = run_bass_kernel_spmd(_get_nc(3), ins, core_ids=cores).results
    out = np.zeros((2, SEQ, D), np.float32)
    for c in range(8):
        b, r = divmod(c, 4)
        o = np.asarray(res[c]["out"])
        for j in range(NSLOT):
            qb = qb_of(r, j)
            out[b, qb * 128:(qb + 1) * 128, :] = o[j]
    return out
```
